# Optimizing a Trainium2 kernel written in Bass

```python
import math
import jax, jax.numpy as jnp
from jax import lax
import numpy as np

D_MODEL = 1024
BATCH = 8
SEQ = 4096
DEPTH = 1

DIFF_QK_DIM = 64
DIFF_V_DIM = 2 * DIFF_QK_DIM
N_DIFF_HEADS = (D_MODEL // 2) // DIFF_V_DIM
FOX_HEAD_DIM = 64
N_FOX_HEADS = (D_MODEL // 2) // FOX_HEAD_DIM
DIFF_QK_W = N_DIFF_HEADS * 2 * DIFF_QK_DIM
DIFF_V_W = N_DIFF_HEADS * DIFF_V_DIM
FOX_W = N_FOX_HEADS * FOX_HEAD_DIM
MIX_W = DIFF_V_W + FOX_W
IN_W = 2 * DIFF_QK_W + DIFF_V_W + 3 * FOX_W + N_FOX_HEADS

N_MEM = 256
N_CROSS_HEADS = 4
CROSS_HEAD_DIM = D_MODEL // N_CROSS_HEADS
D_FF = 4 * D_MODEL
ROPE_THETA = 500000.0
ROT_DIM = DIFF_QK_DIM // 4
Q_BLOCK = 128
EPS = 1e-6
SUBLN_EPS = 1e-5

kernel_name = "hymba_diff_fox_hybrid_layer"


def rmsnorm(x, g, eps=EPS):
    xf = x.astype(jnp.float32)
    y = xf * lax.rsqrt(jnp.mean(xf * xf, axis=-1, keepdims=True) + eps)
    return (y * g.astype(jnp.float32)).astype(x.dtype)


def rope_tables(seq):
    pos = jnp.arange(seq, dtype=jnp.float32)
    inv_freq = ROPE_THETA ** (-jnp.arange(0, ROT_DIM, 2, dtype=jnp.float32) / ROT_DIM)
    ang = pos[:, None] * inv_freq[None, :]
    return jnp.cos(ang), jnp.sin(ang)


def apply_partial_rope(x, cos, sin):
    half = ROT_DIM // 2
    c = cos.astype(x.dtype)
    s = sin.astype(x.dtype)
    x1 = x[..., :half]
    x2 = x[..., half:ROT_DIM]
    return jnp.concatenate([x1 * c - x2 * s, x2 * c + x1 * s, x[..., ROT_DIM:]], axis=-1)


def causal_block_mask(start, end):
    qpos = start + jnp.arange(Q_BLOCK)[:, None]
    kpos = jnp.arange(end)[None, :]
    return kpos <= qpos


def diff_attention(q, k, v, lam):
    seq = q.shape[3]
    scale = DIFF_QK_DIM ** -0.5
    neg = jnp.finfo(jnp.float32).min
    outs = []
    for start in range(0, seq, Q_BLOCK):
        end = start + Q_BLOCK
        s = jnp.einsum('bhmqd,bhmkd->bhmqk', q[:, :, :, start:end], k[:, :, :, :end]).astype(jnp.float32) * scale
        s = jnp.where(causal_block_mask(start, end), s, neg)
        p = jax.nn.softmax(s, axis=-1)
        a = p[:, :, 0] - lam * p[:, :, 1]
        outs.append(jnp.einsum('bhqk,bhkd->bhqd', a.astype(v.dtype), v[:, :, :end]))
    return jnp.concatenate(outs, axis=2)


def forgetting_attention(q, k, v, log_f):
    seq = q.shape[2]
    scale = FOX_HEAD_DIM ** -0.5
    neg = jnp.finfo(jnp.float32).min
    c = jnp.cumsum(log_f, axis=-1)
    outs = []
    for start in range(0, seq, Q_BLOCK):
        end = start + Q_BLOCK
        s = jnp.einsum('bhqd,bhkd->bhqk', q[:, :, start:end], k[:, :, :end]).astype(jnp.float32) * scale
        s = s + c[:, :, start:end, None] - c[:, :, None, :end]
        s = jnp.where(causal_block_mask(start, end), s, neg)
        p = jax.nn.softmax(s, axis=-1)
        outs.append(jnp.einsum('bhqk,bhkd->bhqd', p.astype(v.dtype), v[:, :, :end]))
    return jnp.concatenate(outs, axis=2)


def setup_inputs(seed: int = 0) -> dict:
    key = jax.random.key(seed)
    ks = jax.random.split(key, 24)
    f32 = jnp.float32
    nrm = lambda k, shape, scale: jax.random.normal(k, shape, f32) * scale
    gain = lambda k, shape: 1.0 + 0.02 * jax.random.normal(k, shape, f32)
    return {
        "x": jax.random.normal(ks[0], (BATCH, SEQ, D_MODEL), f32),
        "mem": jax.random.normal(ks[1], (BATCH, N_MEM, D_MODEL), f32),
        "norm_mix_g": gain(ks[2], (DEPTH, D_MODEL)),
        "w_in": nrm(ks[3], (DEPTH, D_MODEL, IN_W), D_MODEL ** -0.5),
        "b_forget": 1.0 + 0.3 * jax.random.normal(ks[4], (DEPTH, N_FOX_HEADS), f32),
        "lam_q1": nrm(ks[5], (DEPTH, DIFF_QK_DIM), 0.1),
        "lam_k1": nrm(ks[6], (DEPTH, DIFF_QK_DIM), 0.1),
        "lam_q2": nrm(ks[7], (DEPTH, DIFF_QK_DIM), 0.1),
        "lam_k2": nrm(ks[8], (DEPTH, DIFF_QK_DIM), 0.1),
        "diff_subln_g": gain(ks[9], (DEPTH, DIFF_V_DIM)),
        "fox_out_g": gain(ks[10], (DEPTH, FOX_HEAD_DIM)),
        "w_out": nrm(ks[11], (DEPTH, MIX_W, D_MODEL), MIX_W ** -0.5),
        "norm_cross_g": gain(ks[12], (DEPTH, D_MODEL)),
        "norm_mem_g": gain(ks[13], (DEPTH, D_MODEL)),
        "w_cq": nrm(ks[14], (DEPTH, D_MODEL, D_MODEL), D_MODEL ** -0.5),
        "w_ckv": nrm(ks[15], (DEPTH, D_MODEL, 2 * D_MODEL), D_MODEL ** -0.5),
        "w_co": nrm(ks[16], (DEPTH, D_MODEL, D_MODEL), D_MODEL ** -0.5),
        "norm_mlp_g": gain(ks[17], (DEPTH, D_MODEL)),
        "w_up": nrm(ks[18], (DEPTH, D_MODEL, D_FF), D_MODEL ** -0.5),
        "w_down": nrm(ks[19], (DEPTH, D_FF, D_MODEL), D_FF ** -0.5),
        "norm_final_g": gain(ks[20], (D_MODEL,)),
    }


def reference(x, mem, norm_mix_g, w_in, b_forget, lam_q1, lam_k1, lam_q2, lam_k2,
              diff_subln_g, fox_out_g, w_out, norm_cross_g, norm_mem_g, w_cq, w_ckv, w_co,
              norm_mlp_g, w_up, w_down, norm_final_g):
    B, S, D = x.shape
    M = mem.shape[1]
    cos, sin = rope_tables(S)
    cos = cos[:, None, None, :]
    sin = sin[:, None, None, :]
    split_pts = [DIFF_QK_W, 2 * DIFF_QK_W, 2 * DIFF_QK_W + DIFF_V_W,
                 2 * DIFF_QK_W + DIFF_V_W + FOX_W, 2 * DIFF_QK_W + DIFF_V_W + 2 * FOX_W,
                 2 * DIFF_QK_W + DIFF_V_W + 3 * FOX_W]
    h = x
    for l in range(DEPTH):
        u = rmsnorm(h, norm_mix_g[l])
        proj = u @ w_in[l]
        dq, dk, dv, fq, fk, fv, fgate = jnp.split(proj, split_pts, axis=-1)

        lambda_init = 0.8 - 0.6 * math.exp(-0.3 * l)
        lam = (jnp.exp(jnp.sum(lam_q1[l].astype(jnp.float32) * lam_k1[l].astype(jnp.float32)))
               - jnp.exp(jnp.sum(lam_q2[l].astype(jnp.float32) * lam_k2[l].astype(jnp.float32)))
               + lambda_init)
        dq = apply_partial_rope(dq.reshape(B, S, N_DIFF_HEADS, 2, DIFF_QK_DIM), cos, sin)
        dk = apply_partial_rope(dk.reshape(B, S, N_DIFF_HEADS, 2, DIFF_QK_DIM), cos, sin)
        dv = dv.reshape(B, S, N_DIFF_HEADS, DIFF_V_DIM).transpose(0, 2, 1, 3)
        d_out = diff_attention(dq.transpose(0, 2, 3, 1, 4), dk.transpose(0, 2, 3, 1, 4), dv, lam)
        d_out = rmsnorm(d_out, diff_subln_g[l], SUBLN_EPS) * (1.0 - lambda_init)
        d_out = d_out.transpose(0, 2, 1, 3).reshape(B, S, DIFF_V_W)

        log_f = jax.nn.log_sigmoid((fgate + b_forget[l]).astype(jnp.float32)).transpose(0, 2, 1)
        to_heads = lambda t: t.reshape(B, S, N_FOX_HEADS, FOX_HEAD_DIM).transpose(0, 2, 1, 3)
        f_out = forgetting_attention(to_heads(fq), to_heads(fk), to_heads(fv), log_f)
        f_out = rmsnorm(f_out, fox_out_g[l])
        f_out = f_out.transpose(0, 2, 1, 3).reshape(B, S, FOX_W)

        h = h + jnp.concatenate([d_out, f_out], axis=-1) @ w_out[l]

        cq = (rmsnorm(h, norm_cross_g[l]) @ w_cq[l]).reshape(B, S, N_CROSS_HEADS, CROSS_HEAD_DIM)
        ckv = rmsnorm(mem, norm_mem_g[l]) @ w_ckv[l]
        ck, cv = jnp.split(ckv, 2, axis=-1)
        ck = ck.reshape(B, M, N_CROSS_HEADS, CROSS_HEAD_DIM)
        cv = cv.reshape(B, M, N_CROSS_HEADS, CROSS_HEAD_DIM)
        cs = jnp.einsum('bshd,bmhd->bhsm', cq, ck).astype(jnp.float32) * (CROSS_HEAD_DIM ** -0.5)
        cp = jax.nn.softmax(cs, axis=-1).astype(cv.dtype)
        co = jnp.einsum('bhsm,bmhd->bshd', cp, cv).reshape(B, S, D)
        h = h + co @ w_co[l]

        z = rmsnorm(h, norm_mlp_g[l]) @ w_up[l]
        h = h + jnp.square(jax.nn.relu(z)) @ w_down[l]
    return rmsnorm(h, norm_final_g)
```

```python
import contextlib
import numpy as np
import ml_dtypes
import concourse.bass as bass
import concourse.mybir as mybir
from concourse.bass_utils import run_bass_kernel_spmd

F32 = mybir.dt.float32
BF16 = mybir.dt.bfloat16
AF = mybir.ActivationFunctionType
ALU = mybir.AluOpType

S = 4096
D = 1024
NT = 32
EPS = 1e-6
SUBLN_EPS = 1e-5
LAMBDA_INIT = 0.8 - 0.6 * 1.0


class Prog:
    def __init__(self):
        self.ops = []
        self.W = {}
        self.R = {}
        self.floor = {}

    def add(self, eng, fn, r=(), w=(), dma=None):
        idx = len(self.ops)
        stream = ('dma', dma) if dma is not None else eng
        deps = set(self.floor.values())
        for k in r:
            d = self.W.get(k)
            if d:
                deps.update(d.values())
        for k in w:
            d = self.W.get(k)
            if d:
                deps.update(d.values())
            d = self.R.get(k)
            if d:
                deps.update(d.values())
        wset = set(w)
        for k in r:
            if k not in wset:
                self.R.setdefault(k, {})[stream] = idx
        for k in w:
            if self.R.get(k):
                self.W[k] = {}
                self.R[k] = {}
            self.W.setdefault(k, {})[stream] = idx
        self.ops.append(dict(eng=eng, fn=fn, deps=deps, dma=dma, stream=stream))
        return idx

    def barrier(self):
        last = {}
        for i, o in enumerate(self.ops):
            last[o['stream']] = i
        self.floor = last

    def mm(self, out, lhsT, rhs, start, stop, r, w, skip=False):
        self.add('pe', lambda e: e.matmul(out, lhsT, rhs, start=start, stop=stop,
                                          skip_group_check=skip), r, w)

    def tr(self, out, in_, ident, r, w):
        self.add('pe', lambda e: e.transpose(out, in_, ident), r, w)

    def act(self, out, in_, func, r, w, bias=None, scale=1.0, accum=None):
        kw = {}
        if bias is not None:
            kw['bias'] = bias
        if accum is not None:
            kw['accum_out'] = accum
        self.add('act', lambda e: e.activation(out, in_, func, scale=scale, **kw), r, w)

    def ts(self, eng, out, in0, s1, s2, op0, op1, r, w, accum=None):
        kw = {}
        if accum is not None:
            kw['accum_out'] = accum
        self.add(eng, lambda e: e.tensor_scalar(out, in0, s1, s2, op0, op1, **kw), r, w)

    def tt(self, eng, out, in0, in1, op, r, w):
        self.add(eng, lambda e: e.tensor_tensor(out, in0, in1, op), r, w)

    def stt(self, out, in0, scalar, in1, op0, op1, r, w, accum=None):
        kw = {}
        if accum is not None:
            kw['accum_out'] = accum
        self.add('dve', lambda e: e.scalar_tensor_tensor(out, in0, scalar, in1, op0, op1, **kw), r, w)

    def cp(self, eng, out, in_, r, w):
        if eng == 'act':
            self.add('act', lambda e: e.activation(out, in_, AF.Copy), r, w)
        else:
            self.add(eng, lambda e: e.tensor_copy(out, in_), r, w)

    def recip(self, out, in_, r, w):
        self.add('dve', lambda e: e.reciprocal(out, in_), r, w)

    def memset(self, eng, ap, val, w):
        self.add(eng, lambda e: e.memset(ap, val), (), w)

    def dma(self, out, in_, sem, r, w, eng='sp'):
        self.add(eng, lambda e: e.dma_start(out=out, in_=in_), r, w, dma=sem)

    def emit(self, nc, stack):
        engs = ['pe', 'act', 'dve', 'pool', 'sp']
        ops = self.ops

        def skipdep(o, od):
            return o['eng'] == 'pe' and od['eng'] == 'pe' and od['dma'] is None and o['dma'] is None

        sig = set()
        for o in ops:
            for d in o['deps']:
                od = ops[d]
                if od['dma'] is None and not skipdep(o, od):
                    sig.add(d)
        semE = {e: stack.enter_context(nc.semaphore('se_' + e)) for e in engs}
        semD = {}
        cnt = {}
        tok = {}
        for i, o in enumerate(ops):
            if o['dma'] is not None:
                k = o['dma']
                if k not in semD:
                    semD[k] = stack.enter_context(nc.semaphore('sd_%d' % len(semD)))
                cnt[('d', k)] = cnt.get(('d', k), 0) + 16
                tok[i] = (('d', k), semD[k], cnt[('d', k)])
            elif i in sig:
                cnt[o['eng']] = cnt.get(o['eng'], 0) + 1
                tok[i] = (o['eng'], semE[o['eng']], cnt[o['eng']])

        def run(ename, e):
            known = {}
            for i, o in enumerate(ops):
                if o['eng'] != ename:
                    continue
                need = {}
                for d in o['deps']:
                    od = ops[d]
                    if skipdep(o, od):
                        continue
                    k, s, v = tok[d]
                    if need.get(k, (None, 0))[1] < v:
                        need[k] = (s, v)
                for k, (s, v) in need.items():
                    if known.get(k, 0) < v:
                        e.wait_ge(s, v)
                        known[k] = v
                if o['fn'] is not None:
                    ins = o['fn'](e)
                    if i in tok:
                        ins.then_inc(tok[i][1], 16 if o['dma'] is not None else 1)

        with nc.Block() as block:
            @block.tensor
            def _(e):
                run('pe', e)

            @block.scalar
            def _(e):
                run('act', e)

            @block.vector
            def _(e):
                run('dve', e)

            @block.gpsimd
            def _(e):
                run('pool', e)

            @block.sync
            def _(e):
                run('sp', e)


class Carve:
    def __init__(self, t, n):
        self.t = t
        self.n = n
        self.off = 0

    def reset(self):
        self.off = 0

    def get(self, nbytes, dtype=BF16, parts=128):
        nb = (nbytes + 31) // 32 * 32
        ne = nb // 2
        assert self.off + ne <= self.n, (self.off, ne, self.n)
        ap = self.t[0:parts, self.off:self.off + nbytes // 2]
        self.off += ne
        if dtype == F32:
            ap = ap.bitcast(F32)
        return ap


def build_nc(stop=None):
    nc = bass.Bass("TRN2", target_bir_lowering=False)

    def din(name, shape, dtype=F32):
        return nc.dram_tensor(name, list(shape), dtype, kind="ExternalInput").ap()

    x = din("x", [S, D])
    mem = din("mem", [256, D])
    w_in = din("w_in", [D, 3080])
    w_out = din("w_out", [D, D])
    w_cq = din("w_cq", [D, D])
    w_ckv = din("w_ckv", [D, 2048])
    w_co = din("w_co", [D, D])
    w_up = din("w_up", [D, 4096])
    w_down = din("w_down", [4096, D])
    gcols_d = din("gcols", [128, 32])
    gsub_d = din("gsub", [128, 2])
    lamv_d = din("lamv", [128, 256])
    bfg_d = din("bfg", [8, 1])
    gfin_d = din("gfin", [128, D])
    cos_d = din("cosT", [128, 512])
    sin_d = din("sinT", [128, 512])
    identb_d = din("identb", [128, 128], BF16)
    identf_d = din("identf", [128, 128])
    negmask_d = din("negmask", [128, 128], BF16)
    y = nc.dram_tensor("y", [S, D], F32, kind="ExternalOutput").ap()
    cps = nc.dram_tensor("cps", [8, 3, S], BF16, kind="Internal").ap()
    cpsn = nc.dram_tensor("cpsn", [8, 3, S], BF16, kind="Internal").ap()
    wcache = nc.dram_tensor("wcache", [22, 128, 4096], BF16, kind="Internal").ap()
    dbg = nc.dram_tensor("dbg", [128, 32768], BF16, kind="ExternalOutput").ap() if stop else None

    stack = contextlib.ExitStack()
    with stack:
        stack.enter_context(nc.allow_low_precision("bf16 matmul operands, fp32 accumulate"))
        stack.enter_context(nc.allow_non_contiguous_dma("small strided loads"))

        def sb(name, shape, dtype):
            return stack.enter_context(nc.sbuf_tensor("sb_" + name, list(shape), dtype))

        def psum(name, shape, dtype):
            return stack.enter_context(nc.psum_tensor(name, list(shape), dtype))

        AT = sb("AT", [128, 8, S], BF16)
        RAt = sb("RA", [128, 32768], BF16)
        NB = 39424
        RBt = sb("RB", [128, NB], BF16)
        identb = sb("identb", [128, 128], BF16)
        identf = sb("identf", [128, 128], F32)
        negmask = sb("negmask", [128, 128], BF16)
        gcols = sb("gcols", [128, 32], F32)
        gsub = sb("gsub", [128, 2], F32)
        small = sb("small", [128, 64], F32)
        onesb = sb("onesb", [128, 2], BF16)
        PSS = [psum("pss%d" % i, [128, 512], F32) for i in range(3)]
        PSA = [psum("psa%d" % i, [128, 512], F32) for i in range(3)]
        PST = [psum("pst%d" % i, [128, 1024], BF16) for i in range(2)]

        neglam = small[:, 0:1]
        negb = small[0:8, 1:2]
        sm_s1 = small[:, 2:3]
        sm_s2 = small[:, 3:4]
        sm_e1 = small[:, 4:5]
        sm_e2 = small[:, 5:6]
        bfg = small[0:8, 6:7]

        p = Prog()

        def dump_finish(ap_bf16_flat):
            p.barrier()
            n = ap_bf16_flat.shape[1]
            p.dma(dbg[0:ap_bf16_flat.shape[0], 0:n], ap_bf16_flat, 'dbgout', (), ['dbgout'])
            p.add('sp', None, ['dbgout'], ())
            p.emit(nc, stack)
            return nc
        RA = Carve(RAt, 32768)
        RB = Carve(RBt, NB)
        rot = {'s': 0, 't': 0, 'p': 0}

        def nextS():
            b = rot['s'] % 3
            rot['s'] += 1
            return PSS[b], ('pss', b)

        def nextT():
            if rot.get('inatt'):
                return PST[0], ('pst', 0)
            b = rot['t'] % 2
            rot['t'] += 1
            return PST[b], ('pst', b)

        uT = RAt[:, :].rearrange('p (c n) -> p c n', c=8)
        lamv = RB.get(256 * 4, F32)
        lamj = RB.get(64 * 4, F32)
        for dst, src in ((identb[:, :], identb_d), (identf[:, :], identf_d), (negmask[:, :], negmask_d),
                         (gcols[:, :], gcols_d), (gsub[:, :], gsub_d), (lamv, lamv_d), (bfg, bfg_d)):
            p.dma(dst, src, 'const', (), ['const'] if src is bfg_d else [])
        p.memset('dve', onesb[:, :], 1.0, ['onesb'])
        p.stt(lamj, lamv[:, 0:64], 1.0, lamv[:, 64:128], ALU.mult, ALU.mult, ['const'], ['lamj', 's1'], accum=sm_s1)
        p.stt(lamj, lamv[:, 128:192], 1.0, lamv[:, 192:256], ALU.mult, ALU.mult, ['const'], ['lamj', 's2'], accum=sm_s2)
        p.act(sm_e1, sm_s1, AF.Exp, ['s1'], ['e1'])
        p.act(sm_e2, sm_s2, AF.Exp, ['s2'], ['e2'])
        p.tt('dve', neglam, sm_e2, sm_e1, ALU.subtract, ['e1', 'e2'], ['neglam'])
        p.ts('dve', neglam, neglam, -LAMBDA_INIT, None, ALU.add, ALU.bypass, ['neglam'], ['neglam'])
        p.ts('dve', negb, bfg, -1.0, None, ALU.mult, ALU.bypass, ['const'], ['negb'])

        xt = RB.get(8 * 1024 * 4, F32).rearrange('p (s n) -> p s n', s=8)
        sqj = RB.get(1024 * 2)
        ubf = RB.get(2 * 1024 * 2).rearrange('p (s n) -> p s n', s=2)
        ss1 = RB.get(32 * 4, F32)
        ms1 = RB.get(32 * 4, F32)
        ln1 = RB.get(32 * 4, F32)
        rs1 = RB.get(32 * 4, F32)
        gq = 0
        def xload(t):
            p.dma(xt[:, t % 8, :], x[t * 128:(t + 1) * 128, :], ('x', t % 8), (), [('xt', t % 8)])
        for t in range(8):
            xload(t)
        def ph1_a(g4):
            tl = range(g4 * 4, g4 * 4 + 4)
            for t in tl:
                p.act(sqj, xt[:, t % 8, :], AF.Square, [('xt', t % 8)], ['sqj', ('ss1', t)], accum=ss1[:, t:t + 1])
            gs = slice(g4 * 4, g4 * 4 + 4)
            p.ts('dve', ms1[:, gs], ss1[:, gs], 1.0 / D, EPS, ALU.mult, ALU.add, [('ss1', t) for t in tl], [('ms1', g4)])
            p.act(ln1[:, gs], ms1[:, gs], AF.Ln, [('ms1', g4)], [('ln1', g4)])
            p.act(rs1[:, gs], ln1[:, gs], AF.Exp, [('ln1', g4)], [('rs1', g4)], scale=-0.5)

        def ph1_b(g4):
            tl = range(g4 * 4, g4 * 4 + 4)
            for t in tl:
                us = t % 2
                p.ts('pool' if t % 2 else 'dve', ubf[:, us, :], xt[:, t % 8, :], rs1[:, t:t + 1], 1.0, ALU.mult, ALU.mult,
                     [('xt', t % 8), ('rs1', g4)], [('ubf', us)])
                T, tk = nextT()
                for c in range(8):
                    p.tr(T[:, c * 128:(c + 1) * 128], ubf[:, us, c * 128:(c + 1) * 128], identb[:, :],
                         [('ubf', us), 'const'], [tk])
                p.cp('act' if t % 2 else 'dve', uT[:, :, t * 128:(t + 1) * 128], T[:, :].rearrange('p (c n) -> p c n', c=8),
                     [tk], [('uT', t)])
            for t in tl:
                if t + 8 < NT:
                    xload(t + 8)

        ph1_a(0)
        for g4 in range(NT // 4):
            if g4 + 1 < NT // 4:
                ph1_a(g4 + 1)
            ph1_b(g4)
        if stop == 'p1':
            return dump_finish(RAt[:, :])
        ATf = AT[:, :, :].rearrange('p c n -> p (c n)')
        cf = ATf[0:8, 0:8192].bitcast(F32)
        negc = ATf[0:8, 8192:16384].bitcast(F32)
        pieces = ATf[0:8, 16384:16384 + 3 * S].rearrange('p (k n) -> p k n', k=3)
        wgst = RB.get(64 * 4, F32).rearrange('p (c n) -> p c n', c=8)
        wgb = RB.get(64 * 2).rearrange('p (c n) -> p c n', c=8)
        p.dma(wgst, w_in[:, 3072:3080].rearrange('(c q) n -> q c n', q=128), 'wg', (), ['wgst'])
        for c in range(8):
            p.ts('dve', wgb[:, c, :], wgst[:, c, :], gcols[:, gq + c:gq + c + 1], 1.0, ALU.mult, ALU.mult,
                 ['wgst', 'const'], [('wgb', c)])
        for n in range(8):
            Sb, sk = nextS()
            for c in range(8):
                p.mm(Sb[0:8, 0:512], wgb[:, c, :], uT[:, c, n * 512:(n + 1) * 512], c == 0, c == 7,
                     [('wgb', c)] + [('uT', n * 4 + i) for i in range(4)], [sk])
            p.act(cf[:, n * 512:(n + 1) * 512], Sb[0:8, 0:512], AF.Exp, [sk, 'negb'], [('cf', n)],
                  bias=negb, scale=-1.0)
            p.act(cf[:, n * 512:(n + 1) * 512], cf[:, n * 512:(n + 1) * 512], AF.Ln, [('cf', n)], [('cf', n)],
                  bias=1.0)
        cfk = [('cf', n) for n in range(8)]
        p.add('dve', lambda e: e.tensor_tensor_scan(negc, onesb[0:8, 0:1].to_broadcast([8, S]), cf, 0.0,
                                                    ALU.mult, ALU.add), cfk + ['onesb'], ['negc'])
        p.ts('dve', pieces[:, 0, :], negc, -1.0, None, ALU.mult, ALU.bypass, ['negc'], ['pc0'])
        p.stt(cf, negc, -1.0, pieces[:, 0, :], ALU.mult, ALU.subtract, ['negc', 'pc0'] + cfk, cfk)
        p.cp('dve', pieces[:, 1, :], cf, cfk, ['pc1'])
        p.tt('dve', cf, cf, pieces[:, 1, :], ALU.subtract, cfk + ['pc1'], cfk)
        p.cp('dve', pieces[:, 2, :], cf, cfk, ['pc2'])
        p.dma(cps, pieces, 'cps', ['pc0', 'pc1', 'pc2'], ['cps'])
        for k3 in range(3):
            p.ts('dve', pieces[:, k3, :], pieces[:, k3, :], -1.0, None, ALU.mult, ALU.bypass, ['pc%d' % k3], ['pc%d' % k3])
        p.dma(cpsn, pieces, 'cpsn', ['pc0', 'pc1', 'pc2'], ['cpsn'])
        p.barrier()

        RB.reset()
        QK = [RB.get(2 * S * 2).rearrange('p (a n) -> p a n', a=2) for _ in range(2)]
        QTa, KTa, QTb, KTb = QK[0][:, 0, :], QK[0][:, 1, :], QK[1][:, 0, :], QK[1][:, 1, :]
        V = RB.get(32 * 130 * 2).rearrange('p (t n) -> p t n', t=32)
        wst = RB.get(4 * 384 * 4, F32).rearrange('p (c n) -> p c n', c=4)
        wg = RB.get(8 * 384 * 2).rearrange('p (c n) -> p c n', c=8)
        PT = [RB.get(512 * 2) for _ in range(5)]
        cosT = RB.get(512 * 4, F32)
        sinT = RB.get(512 * 4, F32)
        dbuf = RB.get(4 * 128 * 4, F32).rearrange('p (t n) -> p t n', t=4)
        dtmp = RB.get(4 * 128 * 4, F32).rearrange('p (t n) -> p t n', t=4)
        atok = RB.get(4 * 128 * 2).rearrange('p (t n) -> p t n', t=4)
        qkrot = RB.get(3 * 256 * 2).rearrange('p (s n) -> p s n', s=3)
        rtmp = RB.get(3 * 4 * 32 * 4, F32).rearrange('p (s k n) -> p s k n', s=3, k=4)
        accsb = RB.get(3 * 387 * 4, F32).rearrange('p (b n) -> p b n', b=3)
        sqb = RB.get(4 * 128 * 4, F32).rearrange('p (t n) -> p t n', t=4)
        rr = RB.get(8 * 4, F32)
        r2n = RB.get(4 * 4, F32)
        ssq = RB.get(4 * 4, F32)
        msq = RB.get(4 * 4, F32)
        lnq = RB.get(4 * 4, F32)
        rsq = RB.get(4 * 4, F32)
        p.dma(cosT, cos_d, 'const2', (), ['const2'])
        p.dma(sinT, sin_d, 'const2', (), ['const2'])
        allQK = [(nm, t) for nm in ('QA', 'KA', 'QB', 'KB') for t in range(NT)]
        Vk = [('V', t) for t in range(NT)]
        p.memset('pool', KTa[64:128, :], 0.0, [('KA', t) for t in range(NT)])
        p.memset('pool', KTb[0:64, :], 0.0, [('KB', t) for t in range(NT)])
        p.memset('pool', V[:, :, 128:129], 1.0, Vk)

        def load_group_w(cols, qscale):
            for half in range(2):
                for j, c0 in enumerate(cols):
                    p.dma(wst[:, :, j * 128:(j + 1) * 128],
                          w_in[half * 512:(half + 1) * 512, c0:c0 + 128].rearrange('(c q) n -> q c n', q=128),
                          ('wst', j), (), [('wst', j)])
                for c in range(4):
                    cc = half * 4 + c
                    g = gcols[:, gq + cc:gq + cc + 1]
                    p.ts('pool', wg[:, cc, 0:128], wst[:, c, 0:128], g, qscale, ALU.mult, ALU.mult,
                         [('wst', 0), 'const'], [('wg', cc)])
                    p.ts('pool', wg[:, cc, 128:384], wst[:, c, 128:384], g, 1.0, ALU.mult, ALU.mult,
                         [('wst', 1), ('wst', 2), 'const'], [('wg', cc)])

        wgk = [('wg', c) for c in range(8)]
        pending = []

        pending2 = []

        def flush1():
            while pending:
                pending.pop(0)()

        def flush2():
            while pending2:
                pending2.pop(0)()

        def flush():
            flush1()
            flush2()

        def attention(streams, nreg_w, post, nbank):
            ns = len(streams)
            steps = [(c, j, si) for c in range(_DEBUG.get('nchunk', 8)) for j in range(4 * c + 4) for si in range(ns)]
            LA = 3
            info = {}
            sbanks = [(PSS[0], ('pss', 0)), (PSS[1], ('pss', 1)), (PSS[2], ('pss', 2)),
                      (PST[1][:, :].bitcast(F32), ('pst', 1))]
            rot['inatt'] = True

            def emitS(k):
                c, j, si = steps[k]
                st = streams[si]
                i = j - 4 * c
                q0 = max(i, 0)
                N = 512 - q0 * 128
                Sb, sk = sbanks[k % 4]
                qk_r = [(st['qn'], c * 4 + t) for t in range(q0, 4)] + [(st['kn'], j)]
                p.mm(Sb[:, 0:N], st['k'][:, j * 128:(j + 1) * 128], st['q'][:, c * 512 + q0 * 128:(c + 1) * 512],
                     True, i < 0, qk_r, [sk])
                if i >= 0:
                    p.mm(Sb[:, 0:128], identb[:, :], negmask[:, :], False, True, ['const'], [sk])
                pi = k % 5
                p.act(PT[pi][:, 0:N], Sb[:, 0:N], AF.Exp, [sk], [('PT', pi)])
                info[k] = (pi, q0)

            def emitPV(k):
                c, j, si = steps[k]
                st = streams[si]
                pi, q0 = info[k]
                for t in range(q0, 4):
                    bank, col = st['acc'](t)
                    first = (j == 0) and st['first'](t)
                    p.mm(PSA[bank][:, col:col + nreg_w], PT[pi][:, (t - q0) * 128:(t - q0 + 1) * 128],
                         st['vsl'](j), first, j == 4 * c + t, [('PT', pi), ('V', j)],
                         [('acc', bank)], skip=True)
                if si == ns - 1 and j == 4:
                    flush1()
                if si == ns - 1 and j == 7:
                    flush2()
                if si == ns - 1 and j == 4 * c + 3:
                    for b, wdt in enumerate(nbank):
                        p.cp('dve', accsb[:, b, 0:wdt], PSA[b][:, 0:wdt], [('acc', b)], [('accsb', b)])
                    post(c)

            for k in range(len(steps) + LA):
                if k < len(steps):
                    emitS(k)
                if k >= LA:
                    emitPV(k - LA)
            flush()
            rot['inatt'] = False

        load_group_w([0, 512, 1024], 0.125)
        for h in range(4):
            def dproj_a(t):
                Sb, sk = nextS()
                for c in range(8):
                    p.mm(Sb[:, 0:384], uT[:, c, t * 128:(t + 1) * 128], wg[:, c, :], c == 0, c == 7,
                         [('uT', t), ('wg', c)], [sk])
                s = t % 3
                pv = Sb[:, 0:256].rearrange('p (a b) -> p a b', a=4)
                cb = cosT[:, t * 16:(t + 1) * 16].rearrange('p (o n) -> p o n', o=1).to_broadcast([128, 4, 16])
                sn = sinT[:, t * 16:(t + 1) * 16].rearrange('p (o n) -> p o n', o=1).to_broadcast([128, 4, 16])
                ta = rtmp[:, s, :, 0:16]
                tb = rtmp[:, s, :, 16:32]
                qr = qkrot[:, s, :].rearrange('p (a b) -> p a b', a=4)
                p.tt('dve', ta, pv[:, :, 0:16], cb, ALU.mult, [sk, 'const2'], [('rt', s, 0)])
                p.tt('dve', tb, pv[:, :, 0:16], sn, ALU.mult, [sk, 'const2'], [('rt', s, 1)])
                p.tt('pool', qr[:, :, 0:8], ta[:, :, 0:8], tb[:, :, 8:16], ALU.subtract, [('rt', s, 0), ('rt', s, 1)],
                     [('qkrot', s)])
                p.tt('pool', qr[:, :, 8:16], ta[:, :, 8:16], tb[:, :, 0:8], ALU.add, [('rt', s, 0), ('rt', s, 1)],
                     [('qkrot', s)])
                p.cp('act', qr[:, :, 16:64], pv[:, :, 16:64], [sk], [('qkrot', s)])
                p.cp('act', V[:, t, 0:128], Sb[:, 256:384], [sk], [('V', t)])

            def dproj_b(t):
                s = t % 3
                T, tk = nextT()
                p.tr(T[:, 0:128], qkrot[:, s, 0:128], identb[:, :], [('qkrot', s), 'const'], [tk])
                p.tr(T[:, 128:256], qkrot[:, s, 128:256], identb[:, :], [('qkrot', s), 'const'], [tk])
                p.cp('act', QTa[:, t * 128:(t + 1) * 128], T[:, 0:128], [tk], [('QA', t)])
                p.cp('dve', KTa[0:64, t * 128:(t + 1) * 128], T[0:64, 128:256], [tk], [('KA', t)])
                p.cp('dve', KTb[64:128, t * 128:(t + 1) * 128], T[64:128, 128:256], [tk], [('KB', t)])

            for t in range(NT + 2):
                if t < NT:
                    dproj_a(t)
                if t >= 2:
                    dproj_b(t - 2)

            if stop == 'p2a':
                return dump_finish(RBt[:, 0:16384])

            def acc_d(s_):
                def f(t):
                    r_ = s_ * 4 + t
                    return (r_ // 3, (r_ % 3) * 129)
                return f

            def first_d(s_):
                return lambda t: (s_ * 4 + t) % 3 == 0

            streams = [dict(q=QTa, k=(KTa if s_ == 0 else KTb), qn='QA', kn=('KA' if s_ == 0 else 'KB'),
                            vsl=(lambda j: V[:, j, 0:129]), bias=None, acc=acc_d(s_), first=first_d(s_))
                       for s_ in range(2)]

            def post_d(c, h=h):
                AX = mybir.AxisListType.X
                accv = accsb[:, :, :].rearrange('p b n -> p (b n)')[:, 0:8 * 129].rearrange('p (r n) -> p r n', r=8)
                ak = [('accsb', b_) for b_ in range(3)]
                dk = [('dbuf', t) for t in range(4)]
                p.recip(rr[:, 0:8], accv[:, :, 128], ak, ['rr'])
                p.ts('dve', r2n[:, 0:4], rr[:, 4:8], neglam, None, ALU.mult, ALU.bypass, ['rr', 'neglam'], ['r2n'])
                bc = lambda v: v.rearrange('p (t o) -> p t o', o=1).to_broadcast([128, 4, 128])
                p.tt('pool', dtmp[:, :, :], accv[:, 4:8, 0:128], bc(r2n[:, 0:4]), ALU.mult, ak + ['r2n'], ['dtmp'])
                p.tt('dve', dbuf[:, :, :], accv[:, 0:4, 0:128], bc(rr[:, 0:4]), ALU.mult, ak + ['rr'], dk)
                p.tt('dve', dbuf[:, :, :], dbuf[:, :, :], dtmp[:, :, :], ALU.add, dk + ['dtmp'], dk)
                p.tt('pool', sqb[:, :, :], dbuf[:, :, :], dbuf[:, :, :], ALU.mult, dk, ['sqb'])
                p.add('dve', lambda e: e.tensor_reduce(ssq[:, 0:4], sqb[:, :, :], AX, ALU.add), ['sqb'], ['ssq'])
                p.ts('dve', msq, ssq, 1.0 / 128, SUBLN_EPS, ALU.mult, ALU.add, ['ssq'], ['msq'])

                def later():
                    p.act(lnq, msq, AF.Ln, ['msq'], ['lnq'])
                    p.act(rsq, lnq, AF.Exp, ['lnq'], ['rsq'], scale=-0.5)
                    for t in range(4):
                        p.ts('pool', atok[:, t, :], dbuf[:, t, :], rsq[:, t:t + 1], 1.0, ALU.mult, ALU.mult,
                             [('dbuf', t), 'rsq'], [('atok', t)])

                def later2():
                    T, tk = nextT()
                    for t in range(4):
                        p.tr(T[:, t * 128:(t + 1) * 128], atok[:, t, :], identb[:, :], [('atok', t), 'const'], [tk])
                    p.cp('dve', AT[:, h, c * 512:(c + 1) * 512], T[:, 0:512], [tk], [('AT', h, c)])
                pending.append(later)
                pending2.append(later2)

            if h < 3:
                load_group_w([(h + 1) * 128, 512 + (h + 1) * 128, 1024 + (h + 1) * 128], 0.125)
            else:
                load_group_w([1536, 2048, 2560], 0.125)
            attention(streams, 129, post_d, [387, 387, 258])
            if stop == 'p2b':
                return dump_finish(ATf)

        p.memset('pool', QTa[64:128, :], 1.0, [('QA', t) for t in range(NT)])
        p.memset('pool', QTb[0:64, :], 1.0, [('QB', t) for t in range(NT)])
        p.memset('pool', KTa[64:67, :], 1.0, [('KA', t) for t in range(NT)])
        p.memset('pool', KTb[0:3, :], 1.0, [('KB', t) for t in range(NT)])
        p.memset('pool', V[:, :, 64:65], 1.0, Vk)
        p.memset('pool', V[:, :, 129:130], 1.0, Vk)
        for pr in range(4):
            p.dma(QTa[64:67, :], cps[2 * pr, :, :], ('cprq', 0), ['cps'], [('QA', t) for t in range(NT)])
            p.dma(KTa[67:70, :], cpsn[2 * pr, :, :], ('cprk', 0), ['cpsn'], [('KA', t) for t in range(NT)])
            p.dma(QTb[0:3, :], cps[2 * pr + 1, :, :], ('cprq', 1), ['cps'], [('QB', t) for t in range(NT)])
            p.dma(KTb[3:6, :], cpsn[2 * pr + 1, :, :], ('cprk', 1), ['cpsn'], [('KB', t) for t in range(NT)])
            for X in range(2):
                dA, dB = (QTa, QTb) if X == 0 else (KTa, KTb)
                nA, nB = ('QA', 'QB') if X == 0 else ('KA', 'KB')
                for n in range(8):
                    Sb, sk = nextS()
                    for c in range(8):
                        p.mm(Sb[:, 0:512], wg[:, c, X * 128:(X + 1) * 128],
                             uT[:, c, n * 512:(n + 1) * 512], c == 0, c == 7,
                             [('wg', c)] + [('uT', n * 4 + i) for i in range(4)], [sk])
                    p.cp('act', dA[0:64, n * 512:(n + 1) * 512], Sb[0:64, 0:512], [sk],
                         [(nA, n * 4 + i) for i in range(4)])
                    p.cp('dve', dB[64:128, n * 512:(n + 1) * 512], Sb[64:128, 0:512], [sk],
                         [(nB, n * 4 + i) for i in range(4)])
            for t in range(NT):
                Sb, sk = nextS()
                for c in range(8):
                    p.mm(Sb[:, 0:128], uT[:, c, t * 128:(t + 1) * 128], wg[:, c, 256:384], c == 0, c == 7,
                         [('uT', t), ('wg', c)], [sk])
                p.cp('dve' if (t % 2) else 'act', V[:, t, 0:130].rearrange('p (a b) -> p a b', a=2)[:, :, 0:64],
                     Sb[:, 0:128].rearrange('p (a b) -> p a b', a=2), [sk], [('V', t)])

            if stop == 'p2c':
                return dump_finish(RBt[:, 0:16384 + 32 * 130])

            def mk_stream(hd, pr=pr):
                Qd, Kd = (QTa, KTa) if hd == 0 else (QTb, KTb)
                head = 2 * pr + hd
                return dict(q=Qd, k=Kd, qn=('QA' if hd == 0 else 'QB'), kn=('KA' if hd == 0 else 'KB'),
                            vsl=(lambda j: V[:, j, hd * 65:(hd + 1) * 65]),
                            bias=None,
                            acc=(lambda t: (hd, t * 65)), first=(lambda t: t == 0))

            def post_f(c, pr=pr):
                AX = mybir.AxisListType.X
                dk = [('dbuf', t) for t in range(4)]
                for hd in range(2):
                    av = accsb[:, hd, 0:260].rearrange('p (t n) -> p t n', t=4)
                    ka = ('accsb', hd)
                    p.recip(rr[:, hd * 4:hd * 4 + 4], av[:, :, 64], [ka], [('rr', hd)])
                    bcr = rr[:, hd * 4:hd * 4 + 4].rearrange('p (t o) -> p t o', o=1).to_broadcast([128, 4, 64])
                    p.tt('dve', dbuf[:, :, hd * 64:(hd + 1) * 64], av[:, :, 0:64], bcr, ALU.mult, [ka, ('rr', hd)], dk)
                p.tt('pool', sqb[:, :, :], dbuf[:, :, :], dbuf[:, :, :], ALU.mult, dk, ['sqb'])
                p.add('dve', lambda e: e.tensor_reduce(ssq8[:, 0:8], sqb[:, :, :].rearrange('p t (a n) -> p t a n', a=2),
                                                       AX, ALU.add), ['sqb'], ['ssq'])
                p.ts('dve', msq8, ssq8, 1.0 / 64, EPS, ALU.mult, ALU.add, ['ssq'], ['msq'])

                def later():
                    p.act(lnq8, msq8, AF.Ln, ['msq'], ['lnq'])
                    p.act(rsq8, lnq8, AF.Exp, ['lnq'], ['rsq'], scale=-0.5)
                    for t in range(4):
                        for hd in range(2):
                            p.ts('pool', atok[:, t, hd * 64:(hd + 1) * 64], dbuf[:, t, hd * 64:(hd + 1) * 64],
                                 rsq8[:, t * 2 + hd:t * 2 + hd + 1], 1.0, ALU.mult, ALU.mult,
                                 [('dbuf', t), 'rsq'], [('atok', t)])

                def later2():
                    T, tk = nextT()
                    for t in range(4):
                        p.tr(T[:, t * 128:(t + 1) * 128], atok[:, t, :], identb[:, :], [('atok', t), 'const'], [tk])
                    p.cp('dve', AT[:, 4 + pr, c * 512:(c + 1) * 512], T[:, 0:512], [tk], [('AT', 4 + pr, c)])
                pending.append(later)
                pending2.append(later2)

            if pr == 0:
                ssq8 = RB.get(8 * 4, F32)
                msq8 = RB.get(8 * 4, F32)
                lnq8 = RB.get(8 * 4, F32)
                rsq8 = RB.get(8 * 4, F32)
            if pr < 3:
                load_group_w([1536 + (pr + 1) * 128, 2048 + (pr + 1) * 128, 2560 + (pr + 1) * 128], 0.125)
            attention([mk_stream(0), mk_stream(1)], 65, post_f, [260, 260])
            if stop == 'p2d':
                return dump_finish(ATf)
        p.barrier()

        if stop == 'p2':
            return dump_finish(ATf)
        RA.reset()
        RB.reset()
        h = RA.get(8 * 1024 * 4, F32).rearrange('p (t n) -> p t n', t=8)
        hnT = RA.get(8 * 1024 * 2).rearrange('p (c n) -> p c n', c=8)
        cqT = RA.get(8 * 1024 * 2).rearrange('p (c n) -> p c n', c=8)
        obuf = cqT[:, :, :].rearrange('p c n -> p (c n)').bitcast(F32).rearrange('p (s n) -> p s n', s=4)
        zT = RB.get(4 * 1024 * 2).rearrange('p (c n) -> p c n', c=4)
        mnT = zT[:, :, :].rearrange('p c n -> p (c n)')[:, 0:2048].rearrange('p (c n) -> p c n', c=8)
        rl = RB.get(1024 * 4, F32)
        rlk = [('rl', 0), ('rl', 1)]
        wst3 = RB.get(8 * 512 * 4, F32)
        wsl = [RB.get(8 * 512 * 2) for _ in range(3)]
        cotok = zT[:, :, :].rearrange('p c n -> p (c n)').rearrange('p (t n) -> p t n', t=4)
        PT3 = [RB.get(512 * 2) for _ in range(4)]
        gfin = RB.get(1024 * 4, F32)
        ckT = RB.get(8 * 256 * 2).rearrange('p (c n) -> p c n', c=8)
        cv = RB.get(2 * 4 * 2 * 129 * 2).rearrange('p (m a v n) -> p m a v n', m=2, a=4, v=2)
        hntok3 = RB.get(3 * 1024 * 2).rearrange('p (s n) -> p s n', s=3)
        hntok = hntok3[:, 0, :]
        ss3 = RB.get(8 * 4, F32)
        ms3 = RB.get(8 * 4, F32)
        ln3 = RB.get(8 * 4, F32)
        rs3 = RB.get(8 * 4, F32)
        rr3 = RB.get(4 * 4, F32)
        ssm = RB.get(8 * 4, F32)
        mjunk = zT[:, :, :].rearrange('p c n -> p (c n)')[:, 2048:3072]
        p.dma(gfin, gfin_d, 'const3', (), ['gfin'])
        wcnt = [0]
        stcnt = [0]
        wst3b = cqT[:, :, :].rearrange('p c n -> p (c n)').bitcast(F32)
        preloaded = {}

        def load_slab(src, shape_c, gfn, cid=None, cached=False, allow_b=True):
            sl = wcnt[0] % 3
            wcnt[0] += 1
            dv = wsl[sl].rearrange('p (c n) -> p c n', c=shape_c)
            if cached:
                p.dma(wsl[sl], wcache[cid, :, :], ('wcr', sl), [('wc', cid)], [('wsl', sl)])
                return dv, ('wsl', sl)
            sb_ = (stcnt[0] % 2) if allow_b else 0
            stcnt[0] += 1
            hc = shape_c // 2
            if sb_ == 0:
                stv = wst3.rearrange('p (c n) -> p c n', c=shape_c)
                hk = [[('wst3', 0)], [('wst3', 1)]]
                sems = [('wst3', 0), ('wst3', 1)]
            else:
                stv = wst3b.rearrange('p (c n) -> p c n', c=shape_c)
                hk = [[('cqT', dt_, n_) for dt_ in range(0, 4) for n_ in range(2)],
                      [('cqT', dt_, n_) for dt_ in range(4, 8) for n_ in range(2)]]
                sems = [('wst3b', 0), ('wst3b', 1)]
            p.dma(stv[:, 0:hc, :], src[:, 0:hc, :], sems[0], (), hk[0])
            p.dma(stv[:, hc:, :], src[:, hc:, :], sems[1], (), hk[1])
            if gfn is None:
                for c in range(shape_c):
                    p.cp('pool', dv[:, c, :], stv[:, c, :], hk[c // hc], [('wsl', sl)])
            else:
                for c in range(shape_c):
                    g1, g2 = gfn(c)
                    p.ts('pool', dv[:, c, :], stv[:, c, :], g1, g2, ALU.mult, ALU.mult, hk[c // hc] + ['const'],
                         [('wsl', sl)])
            if cid is not None:
                p.dma(wcache[cid, :, :], wsl[sl], ('wcw', sl), [('wsl', sl)], [('wc', cid)])
            return dv, ('wsl', sl)

        def gmix(base):
            return lambda c: (gcols[:, base + c:base + c + 1], 1.0)

        def norm_a(tt):
            sl3 = tt % 3
            p.act(rl, h[:, tt, :], AF.Square, [('h', tt)], rlk + [('ss3', tt)], accum=ss3[:, tt:tt + 1])
            p.ts('dve', ms3[:, tt:tt + 1], ss3[:, tt:tt + 1], 1.0 / D, EPS, ALU.mult, ALU.add, [('ss3', tt)], [('ms3', tt)])
            p.act(ln3[:, tt:tt + 1], ms3[:, tt:tt + 1], AF.Ln, [('ms3', tt)], [('ln3', tt)])
            p.act(rs3[:, tt:tt + 1], ln3[:, tt:tt + 1], AF.Exp, [('ln3', tt)], [('rs3', tt)], scale=-0.5)
            p.ts('pool', hntok3[:, sl3, :], h[:, tt, :], rs3[:, tt:tt + 1], 1.0, ALU.mult, ALU.mult,
                 [('h', tt), ('rs3', tt)], [('hntok', sl3)])

        def norm_b(tt):
            sl3 = tt % 3
            T, tk = nextT()
            for c in range(8):
                p.tr(T[:, c * 128:(c + 1) * 128], hntok3[:, sl3, c * 128:(c + 1) * 128], identb[:, :],
                     [('hntok', sl3), 'const'], [tk])
            p.cp('dve' if tt % 2 else 'act', hnT[:, :, tt * 128:(tt + 1) * 128],
                 T[:, :].rearrange('p (c n) -> p c n', c=8), [tk], [('hnT', tt)])

        def norm_hook(tt):
            if tt >= 2:
                norm_b(tt - 2)
            norm_a(tt)
            if tt == 7:
                norm_b(6)
                norm_b(7)

        def rmsnorm_to_T(dstT, dkey):
            for tt in range(8):
                p.act(rl, h[:, tt, :], AF.Square, [('h', tt)], rlk + [('ss3', tt)], accum=ss3[:, tt:tt + 1])
            sk = [('ss3', tt) for tt in range(8)]
            p.ts('dve', ms3, ss3, 1.0 / D, EPS, ALU.mult, ALU.add, sk, ['ms3'])
            p.act(ln3, ms3, AF.Ln, ['ms3'], ['ln3'])
            p.act(rs3, ln3, AF.Exp, ['ln3'], ['rs3'], scale=-0.5)
            for tt in range(8):
                p.ts('pool', hntok, h[:, tt, :], rs3[:, tt:tt + 1], 1.0, ALU.mult, ALU.mult, [('h', tt), 'rs3'], ['hntok'])
                T, tk = nextT()
                for c in range(8):
                    p.tr(T[:, c * 128:(c + 1) * 128], hntok[:, c * 128:(c + 1) * 128], identb[:, :], ['hntok', 'const'], [tk])
                p.cp('dve' if tt % 2 else 'act', dstT[:, :, tt * 128:(tt + 1) * 128],
                     T[:, :].rearrange('p (c n) -> p c n', c=8), [tk], [(dkey, tt)])

        def mem_path():
            for i in range(2):
                p.dma(rl, mem[i * 128:(i + 1) * 128, :], 'memx', (), rlk)
                p.act(mjunk, rl, AF.Square, rlk, [('cotok', t_) for t_ in range(4)] + [('ssm', i)], accum=ssm[:, i:i + 1])
                p.ts('dve', ssm[:, 2 + i:3 + i], ssm[:, i:i + 1], 1.0 / D, EPS, ALU.mult, ALU.add, [('ssm', i)], [('msm', i)])
                p.act(ssm[:, 4 + i:5 + i], ssm[:, 2 + i:3 + i], AF.Ln, [('msm', i)], [('lnm', i)])
                p.act(ssm[:, 6 + i:7 + i], ssm[:, 4 + i:5 + i], AF.Exp, [('lnm', i)], [('rsm', i)], scale=-0.5)
                p.ts('pool', hntok, rl, ssm[:, 6 + i:7 + i], 1.0, ALU.mult, ALU.mult, rlk + [('rsm', i)], [('hntok', 0)])
                T, tk = nextT()
                for c in range(8):
                    p.tr(T[:, c * 128:(c + 1) * 128], hntok[:, c * 128:(c + 1) * 128], identb[:, :], [('hntok', 0), 'const'], [tk])
                p.cp('dve', mnT[:, :, i * 128:(i + 1) * 128], T[:, :].rearrange('p (c n) -> p c n', c=8), [tk], [('mnT', i)])
            p.memset('dve', cv[:, :, :, :, 128:129], 1.0, ['cv'])
            for s4 in range(4):
                wv, wk = load_slab(w_ckv[:, s4 * 512:(s4 + 1) * 512].rearrange('(c q) n -> q c n', q=128), 8, gmix(16))
                if s4 < 2:
                    for dtl in range(4):
                        gd = s4 * 4 + dtl
                        Sb, sk = nextS()
                        for c in range(8):
                            p.mm(Sb[:, 0:256], wv[:, c, dtl * 128:(dtl + 1) * 128], mnT[:, c, :], c == 0, c == 7,
                                 [wk, ('mnT', 0), ('mnT', 1)], [sk])
                        p.ts('dve', ckT[:, gd, :], Sb[:, 0:256], 0.0625, None, ALU.mult, ALU.bypass, [sk], ['ckT'])
                else:
                    hd0 = (s4 - 2) * 2
                    for mt in range(2):
                        Sb, sk = nextS()
                        for c in range(8):
                            p.mm(Sb[:, 0:512], mnT[:, c, mt * 128:(mt + 1) * 128], wv[:, c, :], c == 0, c == 7,
                                 [wk, ('mnT', mt)], [sk])
                        p.cp('dve', cv[:, mt, hd0:hd0 + 2, :, 0:128],
                             Sb[:, 0:512].rearrange('p (a v n) -> p a v n', a=2, v=2), [sk], ['cv'])

        for sc in range(4):
            if sc == 0:
                for tt in range(8):
                    p.dma(h[:, tt, :], x[tt * 128:(tt + 1) * 128, :], ('hx', tt), (), [('h', tt)])

            def proj_add(wsrc, gfn, actT, akey, acol0, cid0, hook=None, allow_b=True):
                for hf in range(2):
                    if (sc, cid0 + hf) in preloaded:
                        wv, wk = preloaded.pop((sc, cid0 + hf))
                    else:
                        wv, wk = load_slab(wsrc[:, hf * 512:(hf + 1) * 512].rearrange('(c q) n -> q c n', q=128), 8, gfn,
                                           cid0 + hf, sc > 0, allow_b=allow_b)
                    for tt in range(8):
                        Sb, sk = nextS()
                        for c in range(8):
                            p.mm(Sb[:, 0:512], actT[:, c, acol0 + tt * 128:acol0 + (tt + 1) * 128], wv[:, c, :],
                                 c == 0, c == 7, [wk] + akey(c, tt), [sk])
                        p.tt('dve', h[:, tt, hf * 512:(hf + 1) * 512], h[:, tt, hf * 512:(hf + 1) * 512], Sb[:, 0:512],
                             ALU.add, [sk, ('h', tt)], [('h', tt)])
                        if hf == 1 and hook is not None:
                            hook(tt)

            proj_add(w_out, (lambda c: (gsub[:, 0:1], 1.0 - LAMBDA_INIT) if c < 4 else (gsub[:, 1:2], 1.0)),
                     AT, (lambda c, tt: [('AT', c, sc * 2 + tt // 4)]), sc * 1024, 0, norm_hook)
            if sc == 0:
                mem_path()
            for hf in range(2):
                wv, wk = load_slab(w_cq[:, hf * 512:(hf + 1) * 512].rearrange('(c q) n -> q c n', q=128), 8, gmix(8),
                                   2 + hf, sc > 0, allow_b=False)
                for dtl in range(4):
                    for n in range(2):
                        Sb, sk = nextS()
                        for c in range(8):
                            p.mm(Sb[:, 0:512], wv[:, c, dtl * 128:(dtl + 1) * 128], hnT[:, c, n * 512:(n + 1) * 512],
                                 c == 0, c == 7, [wk] + [('hnT', n * 4 + i) for i in range(4)], [sk])
                        p.cp('dve' if (dtl + n) % 2 else 'act', cqT[:, hf * 4 + dtl, n * 512:(n + 1) * 512], Sb[:, 0:512],
                             [sk], [('cqT', hf * 4 + dtl, n)])
            units = [(n, hd) for n in range(2) for hd in range(4)]
            upis = {}

            def ca_S(u):
                n, hd = units[u]
                pis = []
                for mt in range(2):
                    Sb, sk = nextS()
                    for dc in range(2):
                        p.mm(Sb[:, 0:512], ckT[:, hd * 2 + dc, mt * 128:(mt + 1) * 128],
                             cqT[:, hd * 2 + dc, n * 512:(n + 1) * 512], dc == 0, dc == 1,
                             ['ckT', ('cqT', hd * 2 + dc, n)], [sk])
                    pi = rot['p'] % 4
                    rot['p'] += 1
                    p.act(PT3[pi], Sb[:, 0:512], AF.Exp, [sk], [('PT3', pi)])
                    pis.append(pi)
                upis[u] = pis

            def ca_PV(u):
                n, hd = units[u]
                pis = upis[u]
                for t in range(4):
                    for hv in range(2):
                        r_ = t * 2 + hv
                        bank, col = r_ // 3, (r_ % 3) * 129
                        for mt in range(2):
                            p.mm(PSA[bank][:, col:col + 129], PT3[pis[mt]][:, t * 128:(t + 1) * 128],
                                 cv[:, mt, hd, hv, :], (mt == 0 and r_ % 3 == 0), mt == 1,
                                 [('PT3', pis[mt]), 'cv'], [('acc', bank)], skip=True)
                for t in range(4):
                    for hv in range(2):
                        r_ = t * 2 + hv
                        bank, col = r_ // 3, (r_ % 3) * 129
                        a_ = PSA[bank][:, col:col + 129]
                        if hv == 0:
                            p.recip(rr3[:, t:t + 1], a_[:, 128:129], [('acc', bank)], [('rr3', t)])
                        p.ts('dve', cotok[:, t, hd * 256 + hv * 128:hd * 256 + (hv + 1) * 128], a_[:, 0:128],
                             rr3[:, t:t + 1], None, ALU.mult, ALU.bypass, [('acc', bank), ('rr3', t)],
                             [('cotok', t)])
                if hd == 3:
                    for t in range(4):
                        T, tk = nextT()
                        for c in range(8):
                            p.tr(T[:, c * 128:(c + 1) * 128], cotok[:, t, c * 128:(c + 1) * 128], identb[:, :],
                                 [('cotok', t), 'const'], [tk])
                        p.cp('act' if t % 2 else 'dve', hnT[:, :, (n * 4 + t) * 128:(n * 4 + t + 1) * 128],
                             T[:, :].rearrange('p (c n) -> p c n', c=8), [tk], [('hnT', n * 4 + t)])

            ca_S(0)
            for u in range(len(units)):
                if u + 1 < len(units):
                    ca_S(u + 1)
                ca_PV(u)
            proj_add(w_co, None, hnT, (lambda c, tt: [('hnT', tt)]), 0, 4, norm_hook, allow_b=False)
            for f in range(8):
                wu, wuk = load_slab(w_up[:, f * 512:(f + 1) * 512].rearrange('(c q) n -> q c n', q=128), 8, gmix(24),
                                    6 + 2 * f, sc > 0)
                wd, wdk = load_slab(w_down[f * 512:(f + 1) * 512, :].rearrange('(c q) n -> q c n', q=128), 4, None,
                                    7 + 2 * f, sc > 0)
                for fi in range(4):
                    for n in range(2):
                        Sb, sk = nextS()
                        for c in range(8):
                            p.mm(Sb[:, 0:512], wu[:, c, fi * 128:(fi + 1) * 128], hnT[:, c, n * 512:(n + 1) * 512],
                                 c == 0, c == 7, [wuk] + [('hnT', n * 4 + i) for i in range(4)], [sk])
                        k2 = (fi * 2 + n) % 2
                        rlv = rl[:, k2 * 512:(k2 + 1) * 512]
                        p.act(rlv, Sb[:, 0:512], AF.Relu, [sk], [('rl', k2)])
                        p.tt('pool' if k2 == 0 else 'dve', zT[:, fi, n * 512:(n + 1) * 512], rlv, rlv, ALU.mult, [('rl', k2)], [('zT', fi, n)])
                if f == 7 and sc < 3:
                    for hf_ in range(2):
                        preloaded[(sc + 1, hf_)] = load_slab(
                            w_out[:, hf_ * 512:(hf_ + 1) * 512].rearrange('(c q) n -> q c n', q=128), 8, None, hf_, True)
                for tt in range(8):
                    for hf in range(2):
                        Sb, sk = nextS()
                        for fi in range(4):
                            p.mm(Sb[:, 0:512], zT[:, fi, tt * 128:(tt + 1) * 128], wd[:, fi, hf * 512:(hf + 1) * 512],
                                 fi == 0, fi == 3, [wdk, ('zT', fi, tt // 4)], [sk])
                        p.tt('dve', h[:, tt, hf * 512:(hf + 1) * 512], h[:, tt, hf * 512:(hf + 1) * 512], Sb[:, 0:512],
                             ALU.add, [sk, ('h', tt)], [('h', tt)])
                        if f == 7 and hf == 1:
                            gt = sc * 8 + tt
                            p.act(rl, h[:, tt, :], AF.Square, [('h', tt)], rlk + [('ss3', tt)], accum=ss3[:, tt:tt + 1])
                            p.ts('dve', ms3[:, tt:tt + 1], ss3[:, tt:tt + 1], 1.0 / D, EPS, ALU.mult, ALU.add,
                                 [('ss3', tt)], [('ms3', tt)])
                            p.act(ln3[:, tt:tt + 1], ms3[:, tt:tt + 1], AF.Ln, [('ms3', tt)], [('ln3', tt)])
                            p.act(rs3[:, tt:tt + 1], ln3[:, tt:tt + 1], AF.Exp, [('ln3', tt)], [('rs3', tt)], scale=-0.5)
                            so = tt % 4
                            ob = obuf[:, so, :]
                            obk = [('cqT', 2 * so + a_, n_) for a_ in range(2) for n_ in range(2)]
                            p.ts('dve', ob, h[:, tt, :], rs3[:, tt:tt + 1], None, ALU.mult, ALU.bypass,
                                 [('h', tt), ('rs3', tt)], obk)
                            p.tt('pool', ob, ob, gfin, ALU.mult, obk + ['gfin'], obk)
                            p.dma(y[gt * 128:(gt + 1) * 128, :], ob, ('out', tt), obk, [('y', gt)])
                            if sc < 3:
                                gn = (sc + 1) * 8 + tt
                                p.dma(h[:, tt, :], x[gn * 128:(gn + 1) * 128, :], ('hx', tt), (), [('h', tt)])
        p.add('sp', None, [('y', gt) for gt in range(NT)], ())
        p.emit(nc, stack)
    return nc


def _consts():
    pos = np.arange(S, dtype=np.float64)
    inv_freq = 500000.0 ** (-np.arange(0, 16, 2, dtype=np.float64) / 16.0)
    ang = pos[:, None] * inv_freq[None, :]
    cos = np.cos(ang).astype(np.float32)
    sin = np.sin(ang).astype(np.float32)
    cos = np.concatenate([cos, cos], axis=1)
    sin = np.concatenate([sin, sin], axis=1)
    cosT = np.ascontiguousarray(cos.reshape(32, 128, 16).transpose(1, 0, 2).reshape(128, 512))
    sinT = np.ascontiguousarray(sin.reshape(32, 128, 16).transpose(1, 0, 2).reshape(128, 512))
    identb = np.eye(128, dtype=np.float32).astype(ml_dtypes.bfloat16)
    identf = np.eye(128, dtype=np.float32)
    k = np.arange(128)[:, None]
    q = np.arange(128)[None, :]
    negmask = np.where(k > q, -30000.0, 0.0).astype(np.float32).astype(ml_dtypes.bfloat16)
    return cosT, sinT, identb, identf, negmask


_NC_CACHE = {}
_DEBUG = {}


def in_maps_fn(shared, x, mem):
    out = []
    for b in range(8):
        m = dict(shared)
        m['x'] = x[b]
        m['mem'] = mem[b]
        out.append(m)
    return out


def kernel(x, mem, norm_mix_g, w_in, b_forget, lam_q1, lam_k1, lam_q2, lam_k2,
           diff_subln_g, fox_out_g, w_out, norm_cross_g, norm_mem_g, w_cq, w_ckv, w_co,
           norm_mlp_g, w_up, w_down, norm_final_g):
    f = lambda a: np.ascontiguousarray(np.asarray(a, dtype=np.float32))
    x = f(x)
    mem = f(mem)
    cosT, sinT, identb, identf, negmask = _consts()
    col = lambda g: f(g).reshape(8, 128).T
    gcols = np.ascontiguousarray(np.concatenate(
        [col(norm_mix_g[0]), col(norm_cross_g[0]), col(norm_mem_g[0]), col(norm_mlp_g[0])], axis=1))
    gsub = np.ascontiguousarray(np.stack([f(diff_subln_g[0]), np.tile(f(fox_out_g[0]), 2)], axis=1))
    lamv = np.ascontiguousarray(np.broadcast_to(
        np.concatenate([f(lam_q1[0]), f(lam_k1[0]), f(lam_q2[0]), f(lam_k2[0])])[None, :], (128, 256)))
    bfg = f(b_forget[0]).reshape(8, 1)
    gfin = np.ascontiguousarray(np.broadcast_to(f(norm_final_g)[None, :], (128, D)))
    shared = dict(w_in=f(w_in[0]), w_out=f(w_out[0]), w_cq=f(w_cq[0]), w_ckv=f(w_ckv[0]), w_co=f(w_co[0]),
                  w_up=f(w_up[0]), w_down=f(w_down[0]), gcols=gcols, gsub=gsub, lamv=lamv, bfg=bfg, gfin=gfin,
                  cosT=cosT, sinT=sinT, identb=identb, identf=identf, negmask=negmask)
    if _DEBUG.get('stop'):
        nc = build_nc(_DEBUG['stop'])
        res = run_bass_kernel_spmd(nc, in_maps_fn(shared, x, mem)[:_DEBUG.get('ncores', 1)],
                                   core_ids=list(range(_DEBUG.get('ncores', 1))))
        return [np.asarray(r["dbg"]) for r in res.results]
    if 'nc' not in _NC_CACHE:
        _NC_CACHE['nc'] = build_nc()
    nc = _NC_CACHE['nc']
    in_maps = []
    for b in range(8):
        m = dict(shared)
        m['x'] = x[b]
        m['mem'] = mem[b]
        in_maps.append(m)
    res = run_bass_kernel_spmd(nc, in_maps, core_ids=list(range(8)))
    return np.stack([np.asarray(r["y"], dtype=np.float32) for r in res.results], axis=0)
```

```python
import contextlib
import numpy as np
import ml_dtypes
import concourse.bass as bass
import concourse.mybir as mybir
from concourse.bass_utils import run_bass_kernel_spmd

F32 = mybir.dt.float32
BF16 = mybir.dt.bfloat16
AF = mybir.ActivationFunctionType
ALU = mybir.AluOpType

S = 4096
D = 1024
NT = 32
EPS = 1e-6
SUBLN_EPS = 1e-5
LAMBDA_INIT = 0.8 - 0.6 * 1.0


class Prog:
    def __init__(self):
        self.ops = []
        self.W = {}
        self.R = {}
        self.floor = {}

    def add(self, eng, fn, r=(), w=(), dma=None):
        idx = len(self.ops)
        stream = ('dma', dma) if dma is not None else eng
        deps = set(self.floor.values())
        for k in r:
            d = self.W.get(k)
            if d:
                deps.update(d.values())
        for k in w:
            d = self.W.get(k)
            if d:
                deps.update(d.values())
            d = self.R.get(k)
            if d:
                deps.update(d.values())
        wset = set(w)
        for k in r:
            if k not in wset:
                self.R.setdefault(k, {})[stream] = idx
        for k in w:
            if self.R.get(k):
                self.W[k] = {}
                self.R[k] = {}
            self.W.setdefault(k, {})[stream] = idx
        self.ops.append(dict(eng=eng, fn=fn, deps=deps, dma=dma, stream=stream))
        return idx

    def barrier(self):
        last = {}
        for i, o in enumerate(self.ops):
            last[o['stream']] = i
        self.floor = last

    def mm(self, out, lhsT, rhs, start, stop, r, w, skip=False):
        self.add('pe', lambda e: e.matmul(out, lhsT, rhs, start=start, stop=stop,
                                          skip_group_check=skip), r, w)

    def tr(self, out, in_, ident, r, w):
        self.add('pe', lambda e: e.transpose(out, in_, ident), r, w)

    def act(self, out, in_, func, r, w, bias=None, scale=1.0, accum=None):
        kw = {}
        if bias is not None:
            kw['bias'] = bias
        if accum is not None:
            kw['accum_out'] = accum
        self.add('act', lambda e: e.activation(out, in_, func, scale=scale, **kw), r, w)

    def ts(self, eng, out, in0, s1, s2, op0, op1, r, w, accum=None):
        kw = {}
        if accum is not None:
            kw['accum_out'] = accum
        self.add(eng, lambda e: e.tensor_scalar(out, in0, s1, s2, op0, op1, **kw), r, w)

    def tt(self, eng, out, in0, in1, op, r, w):
        self.add(eng, lambda e: e.tensor_tensor(out, in0, in1, op), r, w)

    def stt(self, out, in0, scalar, in1, op0, op1, r, w, accum=None):
        kw = {}
        if accum is not None:
            kw['accum_out'] = accum
        self.add('dve', lambda e: e.scalar_tensor_tensor(out, in0, scalar, in1, op0, op1, **kw), r, w)

    def cp(self, eng, out, in_, r, w):
        if eng == 'act':
            self.add('act', lambda e: e.activation(out, in_, AF.Copy), r, w)
        else:
            self.add(eng, lambda e: e.tensor_copy(out, in_), r, w)

    def recip(self, out, in_, r, w):
        self.add('dve', lambda e: e.reciprocal(out, in_), r, w)

    def memset(self, eng, ap, val, w):
        self.add(eng, lambda e: e.memset(ap, val), (), w)

    def dma(self, out, in_, sem, r, w, eng='sp'):
        self.add(eng, lambda e: e.dma_start(out=out, in_=in_), r, w, dma=sem)

    def emit(self, nc, stack):
        engs = ['pe', 'act', 'dve', 'pool', 'sp']
        ops = self.ops

        def skipdep(o, od):
            return o['eng'] == 'pe' and od['eng'] == 'pe' and od['dma'] is None and o['dma'] is None

        sig = set()
        for o in ops:
            for d in o['deps']:
                od = ops[d]
                if od['dma'] is None and not skipdep(o, od):
                    sig.add(d)
        semE = {e: stack.enter_context(nc.semaphore('se_' + e)) for e in engs}
        semD = {}
        cnt = {}
        tok = {}
        for i, o in enumerate(ops):
            if o['dma'] is not None:
                k = o['dma']
                if k not in semD:
                    semD[k] = stack.enter_context(nc.semaphore('sd_%d' % len(semD)))
                cnt[('d', k)] = cnt.get(('d', k), 0) + 16
                tok[i] = (('d', k), semD[k], cnt[('d', k)])
            elif i in sig:
                cnt[o['eng']] = cnt.get(o['eng'], 0) + 1
                tok[i] = (o['eng'], semE[o['eng']], cnt[o['eng']])

        def run(ename, e):
            known = {}
            for i, o in enumerate(ops):
                if o['eng'] != ename:
                    continue
                need = {}
                for d in o['deps']:
                    od = ops[d]
                    if skipdep(o, od):
                        continue
                    k, s, v = tok[d]
                    if need.get(k, (None, 0))[1] < v:
                        need[k] = (s, v)
                for k, (s, v) in need.items():
                    if known.get(k, 0) < v:
                        e.wait_ge(s, v)
                        known[k] = v
                if o['fn'] is not None:
                    ins = o['fn'](e)
                    if i in tok:
                        ins.then_inc(tok[i][1], 16 if o['dma'] is not None else 1)

        with nc.Block() as block:
            @block.tensor
            def _(e):
                run('pe', e)

            @block.scalar
            def _(e):
                run('act', e)

            @block.vector
            def _(e):
                run('dve', e)

            @block.gpsimd
            def _(e):
                run('pool', e)

            @block.sync
            def _(e):
                run('sp', e)


class Carve:
    def __init__(self, t, n):
        self.t = t
        self.n = n
        self.off = 0

    def reset(self):
        self.off = 0

    def get(self, nbytes, dtype=BF16, parts=128):
        nb = (nbytes + 31) // 32 * 32
        ne = nb // 2
        assert self.off + ne <= self.n, (self.off, ne, self.n)
        ap = self.t[0:parts, self.off:self.off + nbytes // 2]
        self.off += ne
        if dtype == F32:
            ap = ap.bitcast(F32)
        return ap


def build_nc(stop=None):
    nc = bass.Bass("TRN2", target_bir_lowering=False)

    def din(name, shape, dtype=F32):
        return nc.dram_tensor(name, list(shape), dtype, kind="ExternalInput").ap()

    x = din("x", [S, D])
    mem = din("mem", [256, D])
    w_in = din("w_in", [D, 3080])
    w_out = din("w_out", [D, D])
    w_cq = din("w_cq", [D, D])
    w_ckv = din("w_ckv", [D, 2048])
    w_co = din("w_co", [D, D])
    w_up = din("w_up", [D, 4096])
    w_down = din("w_down", [4096, D])
    gcols_d = din("gcols", [128, 32])
    gsub_d = din("gsub", [128, 2])
    lamv_d = din("lamv", [128, 256])
    bfg_d = din("bfg", [8, 1])
    gfin_d = din("gfin", [128, D])
    cos_d = din("cosT", [128, 512])
    sin_d = din("sinT", [128, 512])
    identb_d = din("identb", [128, 128], BF16)
    identf_d = din("identf", [128, 128])
    negmask_d = din("negmask", [128, 128], BF16)
    y = nc.dram_tensor("y", [S, D], F32, kind="ExternalOutput").ap()
    cps = nc.dram_tensor("cps", [8, 3, S], BF16, kind="Internal").ap()
    cpsn = nc.dram_tensor("cpsn", [8, 3, S], BF16, kind="Internal").ap()
    wcache = nc.dram_tensor("wcache", [22, 128, 4096], BF16, kind="Internal").ap()
    dbg = nc.dram_tensor("dbg", [128, 32768], BF16, kind="ExternalOutput").ap() if stop else None

    stack = contextlib.ExitStack()
    with stack:
        stack.enter_context(nc.allow_low_precision("bf16 matmul operands, fp32 accumulate"))
        stack.enter_context(nc.allow_non_contiguous_dma("small strided loads"))

        def sb(name, shape, dtype):
            return stack.enter_context(nc.sbuf_tensor("sb_" + name, list(shape), dtype))

        def psum(name, shape, dtype):
            return stack.enter_context(nc.psum_tensor(name, list(shape), dtype))

        AT = sb("AT", [128, 8, S], BF16)
        RAt = sb("RA", [128, 32768], BF16)
        NB = 39424
        RBt = sb("RB", [128, NB], BF16)
        identb = sb("identb", [128, 128], BF16)
        identf = sb("identf", [128, 128], F32)
        negmask = sb("negmask", [128, 128], BF16)
        gcols = sb("gcols", [128, 32], F32)
        gsub = sb("gsub", [128, 2], F32)
        small = sb("small", [128, 64], F32)
        onesb = sb("onesb", [128, 2], BF16)
        PSS = [psum("pss%d" % i, [128, 512], F32) for i in range(3)]
        PSA = [psum("psa%d" % i, [128, 512], F32) for i in range(3)]
        PST = [psum("pst%d" % i, [128, 1024], BF16) for i in range(2)]

        neglam = small[:, 0:1]
        negb = small[0:8, 1:2]
        sm_s1 = small[:, 2:3]
        sm_s2 = small[:, 3:4]
        sm_e1 = small[:, 4:5]
        sm_e2 = small[:, 5:6]
        bfg = small[0:8, 6:7]

        p = Prog()

        def dump_finish(ap_bf16_flat):
            p.barrier()
            n = ap_bf16_flat.shape[1]
            p.dma(dbg[0:ap_bf16_flat.shape[0], 0:n], ap_bf16_flat, 'dbgout', (), ['dbgout'])
            p.add('sp', None, ['dbgout'], ())
            p.emit(nc, stack)
            return nc
        RA = Carve(RAt, 32768)
        RB = Carve(RBt, NB)
        rot = {'s': 0, 't': 0, 'p': 0}

        def nextS():
            b = rot['s'] % 3
            rot['s'] += 1
            return PSS[b], ('pss', b)

        def nextT():
            if rot.get('inatt'):
                return PST[0], ('pst', 0)
            b = rot['t'] % 2
            rot['t'] += 1
            return PST[b], ('pst', b)

        uT = RAt[:, :].rearrange('p (c n) -> p c n', c=8)
        lamv = RB.get(256 * 4, F32)
        lamj = RB.get(64 * 4, F32)
        for dst, src in ((identb[:, :], identb_d), (identf[:, :], identf_d), (negmask[:, :], negmask_d),
                         (gcols[:, :], gcols_d), (gsub[:, :], gsub_d), (lamv, lamv_d), (bfg, bfg_d)):
            p.dma(dst, src, 'const', (), ['const'] if src is bfg_d else [])
        p.memset('dve', onesb[:, :], 1.0, ['onesb'])
        p.stt(lamj, lamv[:, 0:64], 1.0, lamv[:, 64:128], ALU.mult, ALU.mult, ['const'], ['lamj', 's1'], accum=sm_s1)
        p.stt(lamj, lamv[:, 128:192], 1.0, lamv[:, 192:256], ALU.mult, ALU.mult, ['const'], ['lamj', 's2'], accum=sm_s2)
        p.act(sm_e1, sm_s1, AF.Exp, ['s1'], ['e1'])
        p.act(sm_e2, sm_s2, AF.Exp, ['s2'], ['e2'])
        p.tt('dve', neglam, sm_e2, sm_e1, ALU.subtract, ['e1', 'e2'], ['neglam'])
        p.ts('dve', neglam, neglam, -LAMBDA_INIT, None, ALU.add, ALU.bypass, ['neglam'], ['neglam'])
        p.ts('dve', negb, bfg, -1.0, None, ALU.mult, ALU.bypass, ['const'], ['negb'])

        xt = RB.get(8 * 1024 * 4, F32).rearrange('p (s n) -> p s n', s=8)
        sqj = RB.get(1024 * 2)
        ubf = RB.get(2 * 1024 * 2).rearrange('p (s n) -> p s n', s=2)
        ss1 = RB.get(32 * 4, F32)
        ms1 = RB.get(32 * 4, F32)
        ln1 = RB.get(32 * 4, F32)
        rs1 = RB.get(32 * 4, F32)
        gq = 0
        def xload(t):
            p.dma(xt[:, t % 8, :], x[t * 128:(t + 1) * 128, :], ('x', t % 8), (), [('xt', t % 8)])
        for t in range(8):
            xload(t)
        def ph1_a(g4):
            tl = range(g4 * 4, g4 * 4 + 4)
            for t in tl:
                p.act(sqj, xt[:, t % 8, :], AF.Square, [('xt', t % 8)], ['sqj', ('ss1', t)], accum=ss1[:, t:t + 1])
            gs = slice(g4 * 4, g4 * 4 + 4)
            p.ts('dve', ms1[:, gs], ss1[:, gs], 1.0 / D, EPS, ALU.mult, ALU.add, [('ss1', t) for t in tl], [('ms1', g4)])
            p.act(ln1[:, gs], ms1[:, gs], AF.Ln, [('ms1', g4)], [('ln1', g4)])
            p.act(rs1[:, gs], ln1[:, gs], AF.Exp, [('ln1', g4)], [('rs1', g4)], scale=-0.5)

        def ph1_b(g4):
            tl = range(g4 * 4, g4 * 4 + 4)
            for t in tl:
                us = t % 2
                p.ts('pool' if t % 2 else 'dve', ubf[:, us, :], xt[:, t % 8, :], rs1[:, t:t + 1], 1.0, ALU.mult, ALU.mult,
                     [('xt', t % 8), ('rs1', g4)], [('ubf', us)])
                T, tk = nextT()
                for c in range(8):
                    p.tr(T[:, c * 128:(c + 1) * 128], ubf[:, us, c * 128:(c + 1) * 128], identb[:, :],
                         [('ubf', us), 'const'], [tk])
                p.cp('act' if t % 2 else 'dve', uT[:, :, t * 128:(t + 1) * 128], T[:, :].rearrange('p (c n) -> p c n', c=8),
                     [tk], [('uT', t)])
            for t in tl:
                if t + 8 < NT:
                    xload(t + 8)

        ph1_a(0)
        for g4 in range(NT // 4):
            if g4 + 1 < NT // 4:
                ph1_a(g4 + 1)
            ph1_b(g4)
        if stop == 'p1':
            return dump_finish(RAt[:, :])
        ATf = AT[:, :, :].rearrange('p c n -> p (c n)')
        cf = ATf[0:8, 0:8192].bitcast(F32)
        negc = ATf[0:8, 8192:16384].bitcast(F32)
        pieces = ATf[0:8, 16384:16384 + 3 * S].rearrange('p (k n) -> p k n', k=3)
        wgst = RB.get(64 * 4, F32).rearrange('p (c n) -> p c n', c=8)
        wgb = RB.get(64 * 2).rearrange('p (c n) -> p c n', c=8)
        p.dma(wgst, w_in[:, 3072:3080].rearrange('(c q) n -> q c n', q=128), 'wg', (), ['wgst'])
        for c in range(8):
            p.ts('dve', wgb[:, c, :], wgst[:, c, :], gcols[:, gq + c:gq + c + 1], 1.0, ALU.mult, ALU.mult,
                 ['wgst', 'const'], [('wgb', c)])
        for n in range(8):
            Sb, sk = nextS()
            for c in range(8):
                p.mm(Sb[0:8, 0:512], wgb[:, c, :], uT[:, c, n * 512:(n + 1) * 512], c == 0, c == 7,
                     [('wgb', c)] + [('uT', n * 4 + i) for i in range(4)], [sk])
            p.act(cf[:, n * 512:(n + 1) * 512], Sb[0:8, 0:512], AF.Exp, [sk, 'negb'], [('cf', n)],
                  bias=negb, scale=-1.0)
            p.act(cf[:, n * 512:(n + 1) * 512], cf[:, n * 512:(n + 1) * 512], AF.Ln, [('cf', n)], [('cf', n)],
                  bias=1.0)
        cfk = [('cf', n) for n in range(8)]
        p.add('dve', lambda e: e.tensor_tensor_scan(negc, onesb[0:8, 0:1].to_broadcast([8, S]), cf, 0.0,
                                                    ALU.mult, ALU.add), cfk + ['onesb'], ['negc'])
        p.ts('dve', pieces[:, 0, :], negc, -1.0, None, ALU.mult, ALU.bypass, ['negc'], ['pc0'])
        p.stt(cf, negc, -1.0, pieces[:, 0, :], ALU.mult, ALU.subtract, ['negc', 'pc0'] + cfk, cfk)
        p.cp('dve', pieces[:, 1, :], cf, cfk, ['pc1'])
        p.tt('dve', cf, cf, pieces[:, 1, :], ALU.subtract, cfk + ['pc1'], cfk)
        p.cp('dve', pieces[:, 2, :], cf, cfk, ['pc2'])
        p.dma(cps, pieces, 'cps', ['pc0', 'pc1', 'pc2'], ['cps'])
        for k3 in range(3):
            p.ts('dve', pieces[:, k3, :], pieces[:, k3, :], -1.0, None, ALU.mult, ALU.bypass, ['pc%d' % k3], ['pc%d' % k3])
        p.dma(cpsn, pieces, 'cpsn', ['pc0', 'pc1', 'pc2'], ['cpsn'])
        p.barrier()

        RB.reset()
        QK = [RB.get(2 * S * 2).rearrange('p (a n) -> p a n', a=2) for _ in range(2)]
        QTa, KTa, QTb, KTb = QK[0][:, 0, :], QK[0][:, 1, :], QK[1][:, 0, :], QK[1][:, 1, :]
        V = RB.get(32 * 130 * 2).rearrange('p (t n) -> p t n', t=32)
        wst = RB.get(4 * 384 * 4, F32).rearrange('p (c n) -> p c n', c=4)
        wg = RB.get(8 * 384 * 2).rearrange('p (c n) -> p c n', c=8)
        PT = [RB.get(512 * 2) for _ in range(5)]
        cosT = RB.get(512 * 4, F32)
        sinT = RB.get(512 * 4, F32)
        dbuf = RB.get(4 * 128 * 4, F32).rearrange('p (t n) -> p t n', t=4)
        dtmp = RB.get(4 * 128 * 4, F32).rearrange('p (t n) -> p t n', t=4)
        atok = RB.get(4 * 128 * 2).rearrange('p (t n) -> p t n', t=4)
        qkrot = RB.get(3 * 256 * 2).rearrange('p (s n) -> p s n', s=3)
        rtmp = RB.get(3 * 4 * 32 * 4, F32).rearrange('p (s k n) -> p s k n', s=3, k=4)
        accsb = RB.get(3 * 387 * 4, F32).rearrange('p (b n) -> p b n', b=3)
        sqb = RB.get(4 * 128 * 4, F32).rearrange('p (t n) -> p t n', t=4)
        rr = RB.get(8 * 4, F32)
        r2n = RB.get(4 * 4, F32)
        ssq = RB.get(4 * 4, F32)
        msq = RB.get(4 * 4, F32)
        lnq = RB.get(4 * 4, F32)
        rsq = RB.get(4 * 4, F32)
        p.dma(cosT, cos_d, 'const2', (), ['const2'])
        p.dma(sinT, sin_d, 'const2', (), ['const2'])
        allQK = [(nm, t) for nm in ('QA', 'KA', 'QB', 'KB') for t in range(NT)]
        Vk = [('V', t) for t in range(NT)]
        p.memset('pool', KTa[64:128, :], 0.0, [('KA', t) for t in range(NT)])
        p.memset('pool', KTb[0:64, :], 0.0, [('KB', t) for t in range(NT)])
        p.memset('pool', V[:, :, 128:129], 1.0, Vk)

        def load_group_w(cols, qscale):
            for half in range(2):
                for j, c0 in enumerate(cols):
                    p.dma(wst[:, :, j * 128:(j + 1) * 128],
                          w_in[half * 512:(half + 1) * 512, c0:c0 + 128].rearrange('(c q) n -> q c n', q=128),
                          ('wst', j), (), [('wst', j)])
                for c in range(4):
                    cc = half * 4 + c
                    g = gcols[:, gq + cc:gq + cc + 1]
                    p.ts('pool', wg[:, cc, 0:128], wst[:, c, 0:128], g, qscale, ALU.mult, ALU.mult,
                         [('wst', 0), 'const'], [('wg', cc)])
                    p.ts('pool', wg[:, cc, 128:384], wst[:, c, 128:384], g, 1.0, ALU.mult, ALU.mult,
                         [('wst', 1), ('wst', 2), 'const'], [('wg', cc)])

        wgk = [('wg', c) for c in range(8)]
        pending = []

        pending2 = []

        def flush1():
            while pending:
                pending.pop(0)()

        def flush2():
            while pending2:
                pending2.pop(0)()

        def flush():
            flush1()
            flush2()

        def attention(streams, nreg_w, post, nbank):
            ns = len(streams)
            steps = [(c, j, si) for c in range(_DEBUG.get('nchunk', 8)) for j in range(4 * c + 4) for si in range(ns)]
            LA = 3
            info = {}
            sbanks = [(PSS[0], ('pss', 0)), (PSS[1], ('pss', 1)), (PSS[2], ('pss', 2)),
                      (PST[1][:, :].bitcast(F32), ('pst', 1))]
            rot['inatt'] = True

            def emitS(k):
                c, j, si = steps[k]
                st = streams[si]
                i = j - 4 * c
                q0 = max(i, 0)
                N = 512 - q0 * 128
                Sb, sk = sbanks[k % 4]
                qk_r = [(st['qn'], c * 4 + t) for t in range(q0, 4)] + [(st['kn'], j)]
                p.mm(Sb[:, 0:N], st['k'][:, j * 128:(j + 1) * 128], st['q'][:, c * 512 + q0 * 128:(c + 1) * 512],
                     True, i < 0, qk_r, [sk])
                if i >= 0:
                    p.mm(Sb[:, 0:128], identb[:, :], negmask[:, :], False, True, ['const'], [sk])
                pi = k % 5
                p.act(PT[pi][:, 0:N], Sb[:, 0:N], AF.Exp, [sk], [('PT', pi)])
                info[k] = (pi, q0)

            def emitPV(k):
                c, j, si = steps[k]
                st = streams[si]
                pi, q0 = info[k]
                for t in range(q0, 4):
                    bank, col = st['acc'](t)
                    first = (j == 0) and st['first'](t)
                    p.mm(PSA[bank][:, col:col + nreg_w], PT[pi][:, (t - q0) * 128:(t - q0 + 1) * 128],
                         st['vsl'](j), first, j == 4 * c + t, [('PT', pi), ('V', j)],
                         [('acc', bank)], skip=True)
                if si == ns - 1 and j == 4:
                    flush1()
                if si == ns - 1 and j == 7:
                    flush2()
                if si == ns - 1 and j == 4 * c + 3:
                    for b, wdt in enumerate(nbank):
                        p.cp('dve', accsb[:, b, 0:wdt], PSA[b][:, 0:wdt], [('acc', b)], [('accsb', b)])
                    post(c)

            for k in range(len(steps) + LA):
                if k < len(steps):
                    emitS(k)
                if k >= LA:
                    emitPV(k - LA)
            flush()
            rot['inatt'] = False

        load_group_w([0, 512, 1024], 0.125)
        for h in range(4):
            def dproj_a(t):
                Sb, sk = nextS()
                for c in range(8):
                    p.mm(Sb[:, 0:384], uT[:, c, t * 128:(t + 1) * 128], wg[:, c, :], c == 0, c == 7,
                         [('uT', t), ('wg', c)], [sk])
                s = t % 3
                pv = Sb[:, 0:256].rearrange('p (a b) -> p a b', a=4)
                cb = cosT[:, t * 16:(t + 1) * 16].rearrange('p (o n) -> p o n', o=1).to_broadcast([128, 4, 16])
                sn = sinT[:, t * 16:(t + 1) * 16].rearrange('p (o n) -> p o n', o=1).to_broadcast([128, 4, 16])
                ta = rtmp[:, s, :, 0:16]
                tb = rtmp[:, s, :, 16:32]
                qr = qkrot[:, s, :].rearrange('p (a b) -> p a b', a=4)
                p.tt('dve', ta, pv[:, :, 0:16], cb, ALU.mult, [sk, 'const2'], [('rt', s, 0)])
                p.tt('dve', tb, pv[:, :, 0:16], sn, ALU.mult, [sk, 'const2'], [('rt', s, 1)])
                p.tt('pool', qr[:, :, 0:8], ta[:, :, 0:8], tb[:, :, 8:16], ALU.subtract, [('rt', s, 0), ('rt', s, 1)],
                     [('qkrot', s)])
                p.tt('pool', qr[:, :, 8:16], ta[:, :, 8:16], tb[:, :, 0:8], ALU.add, [('rt', s, 0), ('rt', s, 1)],
                     [('qkrot', s)])
                p.cp('act', qr[:, :, 16:64], pv[:, :, 16:64], [sk], [('qkrot', s)])
                p.cp('act', V[:, t, 0:128], Sb[:, 256:384], [sk], [('V', t)])

            def dproj_b(t):
                s = t % 3
                T, tk = nextT()
                p.tr(T[:, 0:128], qkrot[:, s, 0:128], identb[:, :], [('qkrot', s), 'const'], [tk])
                p.tr(T[:, 128:256], qkrot[:, s, 128:256], identb[:, :], [('qkrot', s), 'const'], [tk])
                p.cp('act', QTa[:, t * 128:(t + 1) * 128], T[:, 0:128], [tk], [('QA', t)])
                p.cp('dve', KTa[0:64, t * 128:(t + 1) * 128], T[0:64, 128:256], [tk], [('KA', t)])
                p.cp('dve', KTb[64:128, t * 128:(t + 1) * 128], T[64:128, 128:256], [tk], [('KB', t)])

            for t in range(NT + 2):
                if t < NT:
                    dproj_a(t)
                if t >= 2:
                    dproj_b(t - 2)

            if stop == 'p2a':
                return dump_finish(RBt[:, 0:16384])

            def acc_d(s_):
                def f(t):
                    r_ = s_ * 4 + t
                    return (r_ // 3, (r_ % 3) * 129)
                return f

            def first_d(s_):
                return lambda t: (s_ * 4 + t) % 3 == 0

            streams = [dict(q=QTa, k=(KTa if s_ == 0 else KTb), qn='QA', kn=('KA' if s_ == 0 else 'KB'),
                            vsl=(lambda j: V[:, j, 0:129]), bias=None, acc=acc_d(s_), first=first_d(s_))
                       for s_ in range(2)]

            def post_d(c, h=h):
                AX = mybir.AxisListType.X
                accv = accsb[:, :, :].rearrange('p b n -> p (b n)')[:, 0:8 * 129].rearrange('p (r n) -> p r n', r=8)
                ak = [('accsb', b_) for b_ in range(3)]
                dk = [('dbuf', t) for t in range(4)]
                p.recip(rr[:, 0:8], accv[:, :, 128], ak, ['rr'])
                p.ts('dve', r2n[:, 0:4], rr[:, 4:8], neglam, None, ALU.mult, ALU.bypass, ['rr', 'neglam'], ['r2n'])
                bc = lambda v: v.rearrange('p (t o) -> p t o', o=1).to_broadcast([128, 4, 128])
                p.tt('pool', dtmp[:, :, :], accv[:, 4:8, 0:128], bc(r2n[:, 0:4]), ALU.mult, ak + ['r2n'], ['dtmp'])
                p.tt('dve', dbuf[:, :, :], accv[:, 0:4, 0:128], bc(rr[:, 0:4]), ALU.mult, ak + ['rr'], dk)
                p.tt('dve', dbuf[:, :, :], dbuf[:, :, :], dtmp[:, :, :], ALU.add, dk + ['dtmp'], dk)
                p.tt('pool', sqb[:, :, :], dbuf[:, :, :], dbuf[:, :, :], ALU.mult, dk, ['sqb'])
                p.add('dve', lambda e: e.tensor_reduce(ssq[:, 0:4], sqb[:, :, :], AX, ALU.add), ['sqb'], ['ssq'])
                p.ts('dve', msq, ssq, 1.0 / 128, SUBLN_EPS, ALU.mult, ALU.add, ['ssq'], ['msq'])

                def later():
                    p.act(lnq, msq, AF.Ln, ['msq'], ['lnq'])
                    p.act(rsq, lnq, AF.Exp, ['lnq'], ['rsq'], scale=-0.5)
                    for t in range(4):
                        p.ts('pool', atok[:, t, :], dbuf[:, t, :], rsq[:, t:t + 1], 1.0, ALU.mult, ALU.mult,
                             [('dbuf', t), 'rsq'], [('atok', t)])

                def later2():
                    T, tk = nextT()
                    for t in range(4):
                        p.tr(T[:, t * 128:(t + 1) * 128], atok[:, t, :], identb[:, :], [('atok', t), 'const'], [tk])
                    p.cp('dve', AT[:, h, c * 512:(c + 1) * 512], T[:, 0:512], [tk], [('AT', h, c)])
                pending.append(later)
                pending2.append(later2)

            if h < 3:
                load_group_w([(h + 1) * 128, 512 + (h + 1) * 128, 1024 + (h + 1) * 128], 0.125)
            else:
                load_group_w([1536, 2048, 2560], 0.125)
            attention(streams, 129, post_d, [387, 387, 258])
            if stop == 'p2b':
                return dump_finish(ATf)

        p.memset('pool', QTa[64:128, :], 1.0, [('QA', t) for t in range(NT)])
        p.memset('pool', QTb[0:64, :], 1.0, [('QB', t) for t in range(NT)])
        p.memset('pool', KTa[64:67, :], 1.0, [('KA', t) for t in range(NT)])
        p.memset('pool', KTb[0:3, :], 1.0, [('KB', t) for t in range(NT)])
        p.memset('pool', V[:, :, 64:65], 1.0, Vk)
        p.memset('pool', V[:, :, 129:130], 1.0, Vk)
        for pr in range(4):
            p.dma(QTa[64:67, :], cps[2 * pr, :, :], ('cprq', 0), ['cps'], [('QA', t) for t in range(NT)])
            p.dma(KTa[67:70, :], cpsn[2 * pr, :, :], ('cprk', 0), ['cpsn'], [('KA', t) for t in range(NT)])
            p.dma(QTb[0:3, :], cps[2 * pr + 1, :, :], ('cprq', 1), ['cps'], [('QB', t) for t in range(NT)])
            p.dma(KTb[3:6, :], cpsn[2 * pr + 1, :, :], ('cprk', 1), ['cpsn'], [('KB', t) for t in range(NT)])
            for X in range(2):
                dA, dB = (QTa, QTb) if X == 0 else (KTa, KTb)
                nA, nB = ('QA', 'QB') if X == 0 else ('KA', 'KB')
                for n in range(8):
                    Sb, sk = nextS()
                    for c in range(8):
                        p.mm(Sb[:, 0:512], wg[:, c, X * 128:(X + 1) * 128],
                             uT[:, c, n * 512:(n + 1) * 512], c == 0, c == 7,
                             [('wg', c)] + [('uT', n * 4 + i) for i in range(4)], [sk])
                    p.cp('act', dA[0:64, n * 512:(n + 1) * 512], Sb[0:64, 0:512], [sk],
                         [(nA, n * 4 + i) for i in range(4)])
                    p.cp('dve', dB[64:128, n * 512:(n + 1) * 512], Sb[64:128, 0:512], [sk],
                         [(nB, n * 4 + i) for i in range(4)])
            for t in range(NT):
                Sb, sk = nextS()
                for c in range(8):
                    p.mm(Sb[:, 0:128], uT[:, c, t * 128:(t + 1) * 128], wg[:, c, 256:384], c == 0, c == 7,
                         [('uT', t), ('wg', c)], [sk])
                p.cp('dve' if (t % 2) else 'act', V[:, t, 0:130].rearrange('p (a b) -> p a b', a=2)[:, :, 0:64],
                     Sb[:, 0:128].rearrange('p (a b) -> p a b', a=2), [sk], [('V', t)])

            if stop == 'p2c':
                return dump_finish(RBt[:, 0:16384 + 32 * 130])

            def mk_stream(hd, pr=pr):
                Qd, Kd = (QTa, KTa) if hd == 0 else (QTb, KTb)
                head = 2 * pr + hd
                return dict(q=Qd, k=Kd, qn=('QA' if hd == 0 else 'QB'), kn=('KA' if hd == 0 else 'KB'),
                            vsl=(lambda j: V[:, j, hd * 65:(hd + 1) * 65]),
                            bias=None,
                            acc=(lambda t: (hd, t * 65)), first=(lambda t: t == 0))

            def post_f(c, pr=pr):
                AX = mybir.AxisListType.X
                dk = [('dbuf', t) for t in range(4)]
                for hd in range(2):
                    av = accsb[:, hd, 0:260].rearrange('p (t n) -> p t n', t=4)
                    ka = ('accsb', hd)
                    p.recip(rr[:, hd * 4:hd * 4 + 4], av[:, :, 64], [ka], [('rr', hd)])
                    bcr = rr[:, hd * 4:hd * 4 + 4].rearrange('p (t o) -> p t o', o=1).to_broadcast([128, 4, 64])
                    p.tt('dve', dbuf[:, :, hd * 64:(hd + 1) * 64], av[:, :, 0:64], bcr, ALU.mult, [ka, ('rr', hd)], dk)
                p.tt('pool', sqb[:, :, :], dbuf[:, :, :], dbuf[:, :, :], ALU.mult, dk, ['sqb'])
                p.add('dve', lambda e: e.tensor_reduce(ssq8[:, 0:8], sqb[:, :, :].rearrange('p t (a n) -> p t a n', a=2),
                                                       AX, ALU.add), ['sqb'], ['ssq'])
                p.ts('dve', msq8, ssq8, 1.0 / 64, EPS, ALU.mult, ALU.add, ['ssq'], ['msq'])

                def later():
                    p.act(lnq8, msq8, AF.Ln, ['msq'], ['lnq'])
                    p.act(rsq8, lnq8, AF.Exp, ['lnq'], ['rsq'], scale=-0.5)
                    for t in range(4):
                        for hd in range(2):
                            p.ts('pool', atok[:, t, hd * 64:(hd + 1) * 64], dbuf[:, t, hd * 64:(hd + 1) * 64],
                                 rsq8[:, t * 2 + hd:t * 2 + hd + 1], 1.0, ALU.mult, ALU.mult,
                                 [('dbuf', t), 'rsq'], [('atok', t)])

                def later2():
                    T, tk = nextT()
                    for t in range(4):
                        p.tr(T[:, t * 128:(t + 1) * 128], atok[:, t, :], identb[:, :], [('atok', t), 'const'], [tk])
                    p.cp('dve', AT[:, 4 + pr, c * 512:(c + 1) * 512], T[:, 0:512], [tk], [('AT', 4 + pr, c)])
                pending.append(later)
                pending2.append(later2)

            if pr == 0:
                ssq8 = RB.get(8 * 4, F32)
                msq8 = RB.get(8 * 4, F32)
                lnq8 = RB.get(8 * 4, F32)
                rsq8 = RB.get(8 * 4, F32)
            if pr < 3:
                load_group_w([1536 + (pr + 1) * 128, 2048 + (pr + 1) * 128, 2560 + (pr + 1) * 128], 0.125)
            attention([mk_stream(0), mk_stream(1)], 65, post_f, [260, 260])
            if stop == 'p2d':
                return dump_finish(ATf)
        p.barrier()

        if stop == 'p2':
            return dump_finish(ATf)
        RA.reset()
        RB.reset()
        h = RA.get(8 * 1024 * 4, F32).rearrange('p (t n) -> p t n', t=8)
        hnT = RA.get(8 * 1024 * 2).rearrange('p (c n) -> p c n', c=8)
        cqT = RA.get(8 * 1024 * 2).rearrange('p (c n) -> p c n', c=8)
        obuf = cqT[:, :, :].rearrange('p c n -> p (c n)').bitcast(F32).rearrange('p (s n) -> p s n', s=4)
        zT = RB.get(4 * 1024 * 2).rearrange('p (c n) -> p c n', c=4)
        mnT = zT[:, :, :].rearrange('p c n -> p (c n)')[:, 0:2048].rearrange('p (c n) -> p c n', c=8)
        rl = RB.get(1024 * 4, F32)
        rlk = [('rl', 0), ('rl', 1)]
        wst3 = RB.get(8 * 512 * 4, F32)
        wsl = [RB.get(8 * 512 * 2) for _ in range(3)]
        cotok = zT[:, :, :].rearrange('p c n -> p (c n)').rearrange('p (t n) -> p t n', t=4)
        PT3 = [RB.get(512 * 2) for _ in range(4)]
        gfin = RB.get(1024 * 4, F32)
        ckT = RB.get(8 * 256 * 2).rearrange('p (c n) -> p c n', c=8)
        cv = RB.get(2 * 4 * 2 * 129 * 2).rearrange('p (m a v n) -> p m a v n', m=2, a=4, v=2)
        hntok3 = RB.get(3 * 1024 * 2).rearrange('p (s n) -> p s n', s=3)
        hntok = hntok3[:, 0, :]
        ss3 = RB.get(8 * 4, F32)
        ms3 = RB.get(8 * 4, F32)
        ln3 = RB.get(8 * 4, F32)
        rs3 = RB.get(8 * 4, F32)
        rr3 = RB.get(4 * 4, F32)
        ssm = RB.get(8 * 4, F32)
        mjunk = zT[:, :, :].rearrange('p c n -> p (c n)')[:, 2048:3072]
        p.dma(gfin, gfin_d, 'const3', (), ['gfin'])
        wcnt = [0]
        stcnt = [0]
        wst3b = cqT[:, :, :].rearrange('p c n -> p (c n)').bitcast(F32)
        preloaded = {}

        def load_slab(src, shape_c, gfn, cid=None, cached=False, allow_b=True):
            sl = wcnt[0] % 3
            wcnt[0] += 1
            dv = wsl[sl].rearrange('p (c n) -> p c n', c=shape_c)
            if cached:
                p.dma(wsl[sl], wcache[cid, :, :], ('wcr', sl), [('wc', cid)], [('wsl', sl)])
                return dv, ('wsl', sl)
            sb_ = (stcnt[0] % 2) if allow_b else 0
            stcnt[0] += 1
            hc = shape_c // 2
            if sb_ == 0:
                stv = wst3.rearrange('p (c n) -> p c n', c=shape_c)
                hk = [[('wst3', 0)], [('wst3', 1)]]
                sems = [('wst3', 0), ('wst3', 1)]
            else:
                stv = wst3b.rearrange('p (c n) -> p c n', c=shape_c)
                hk = [[('cqT', dt_, n_) for dt_ in range(0, 4) for n_ in range(2)],
                      [('cqT', dt_, n_) for dt_ in range(4, 8) for n_ in range(2)]]
                sems = [('wst3b', 0), ('wst3b', 1)]
            p.dma(stv[:, 0:hc, :], src[:, 0:hc, :], sems[0], (), hk[0])
            p.dma(stv[:, hc:, :], src[:, hc:, :], sems[1], (), hk[1])
            if gfn is None:
                for c in range(shape_c):
                    p.cp('pool', dv[:, c, :], stv[:, c, :], hk[c // hc], [('wsl', sl)])
            else:
                for c in range(shape_c):
                    g1, g2 = gfn(c)
                    p.ts('pool', dv[:, c, :], stv[:, c, :], g1, g2, ALU.mult, ALU.mult, hk[c // hc] + ['const'],
                         [('wsl', sl)])
            if cid is not None:
                p.dma(wcache[cid, :, :], wsl[sl], ('wcw', sl), [('wsl', sl)], [('wc', cid)])
            return dv, ('wsl', sl)

        def gmix(base):
            return lambda c: (gcols[:, base + c:base + c + 1], 1.0)

        def norm_a(tt):
            sl3 = tt % 3
            p.act(rl, h[:, tt, :], AF.Square, [('h', tt)], rlk + [('ss3', tt)], accum=ss3[:, tt:tt + 1])
            p.ts('dve', ms3[:, tt:tt + 1], ss3[:, tt:tt + 1], 1.0 / D, EPS, ALU.mult, ALU.add, [('ss3', tt)], [('ms3', tt)])
            p.act(ln3[:, tt:tt + 1], ms3[:, tt:tt + 1], AF.Ln, [('ms3', tt)], [('ln3', tt)])
            p.act(rs3[:, tt:tt + 1], ln3[:, tt:tt + 1], AF.Exp, [('ln3', tt)], [('rs3', tt)], scale=-0.5)
            p.ts('pool', hntok3[:, sl3, :], h[:, tt, :], rs3[:, tt:tt + 1], 1.0, ALU.mult, ALU.mult,
                 [('h', tt), ('rs3', tt)], [('hntok', sl3)])

        def norm_b(tt):
            sl3 = tt % 3
            T, tk = nextT()
            for c in range(8):
                p.tr(T[:, c * 128:(c + 1) * 128], hntok3[:, sl3, c * 128:(c + 1) * 128], identb[:, :],
                     [('hntok', sl3), 'const'], [tk])
            p.cp('dve' if tt % 2 else 'act', hnT[:, :, tt * 128:(tt + 1) * 128],
                 T[:, :].rearrange('p (c n) -> p c n', c=8), [tk], [('hnT', tt)])

        def norm_hook(tt):
            if tt >= 2:
                norm_b(tt - 2)
            norm_a(tt)
            if tt == 7:
                norm_b(6)
                norm_b(7)

        def rmsnorm_to_T(dstT, dkey):
            for tt in range(8):
                p.act(rl, h[:, tt, :], AF.Square, [('h', tt)], rlk + [('ss3', tt)], accum=ss3[:, tt:tt + 1])
            sk = [('ss3', tt) for tt in range(8)]
            p.ts('dve', ms3, ss3, 1.0 / D, EPS, ALU.mult, ALU.add, sk, ['ms3'])
            p.act(ln3, ms3, AF.Ln, ['ms3'], ['ln3'])
            p.act(rs3, ln3, AF.Exp, ['ln3'], ['rs3'], scale=-0.5)
            for tt in range(8):
                p.ts('pool', hntok, h[:, tt, :], rs3[:, tt:tt + 1], 1.0, ALU.mult, ALU.mult, [('h', tt), 'rs3'], ['hntok'])
                T, tk = nextT()
                for c in range(8):
                    p.tr(T[:, c * 128:(c + 1) * 128], hntok[:, c * 128:(c + 1) * 128], identb[:, :], ['hntok', 'const'], [tk])
                p.cp('dve' if tt % 2 else 'act', dstT[:, :, tt * 128:(tt + 1) * 128],
                     T[:, :].rearrange('p (c n) -> p c n', c=8), [tk], [(dkey, tt)])

        def mem_path():
            for i in range(2):
                p.dma(rl, mem[i * 128:(i + 1) * 128, :], 'memx', (), rlk)
                p.act(mjunk, rl, AF.Square, rlk, [('cotok', t_) for t_ in range(4)] + [('ssm', i)], accum=ssm[:, i:i + 1])
                p.ts('dve', ssm[:, 2 + i:3 + i], ssm[:, i:i + 1], 1.0 / D, EPS, ALU.mult, ALU.add, [('ssm', i)], [('msm', i)])
                p.act(ssm[:, 4 + i:5 + i], ssm[:, 2 + i:3 + i], AF.Ln, [('msm', i)], [('lnm', i)])
                p.act(ssm[:, 6 + i:7 + i], ssm[:, 4 + i:5 + i], AF.Exp, [('lnm', i)], [('rsm', i)], scale=-0.5)
                p.ts('pool', hntok, rl, ssm[:, 6 + i:7 + i], 1.0, ALU.mult, ALU.mult, rlk + [('rsm', i)], [('hntok', 0)])
                T, tk = nextT()
                for c in range(8):
                    p.tr(T[:, c * 128:(c + 1) * 128], hntok[:, c * 128:(c + 1) * 128], identb[:, :], [('hntok', 0), 'const'], [tk])
                p.cp('dve', mnT[:, :, i * 128:(i + 1) * 128], T[:, :].rearrange('p (c n) -> p c n', c=8), [tk], [('mnT', i)])
            p.memset('dve', cv[:, :, :, :, 128:129], 1.0, ['cv'])
            for s4 in range(4):
                wv, wk = load_slab(w_ckv[:, s4 * 512:(s4 + 1) * 512].rearrange('(c q) n -> q c n', q=128), 8, gmix(16))
                if s4 < 2:
                    for dtl in range(4):
                        gd = s4 * 4 + dtl
                        Sb, sk = nextS()
                        for c in range(8):
                            p.mm(Sb[:, 0:256], wv[:, c, dtl * 128:(dtl + 1) * 128], mnT[:, c, :], c == 0, c == 7,
                                 [wk, ('mnT', 0), ('mnT', 1)], [sk])
                        p.ts('dve', ckT[:, gd, :], Sb[:, 0:256], 0.0625, None, ALU.mult, ALU.bypass, [sk], ['ckT'])
                else:
                    hd0 = (s4 - 2) * 2
                    for mt in range(2):
                        Sb, sk = nextS()
                        for c in range(8):
                            p.mm(Sb[:, 0:512], mnT[:, c, mt * 128:(mt + 1) * 128], wv[:, c, :], c == 0, c == 7,
                                 [wk, ('mnT', mt)], [sk])
                        p.cp('dve', cv[:, mt, hd0:hd0 + 2, :, 0:128],
                             Sb[:, 0:512].rearrange('p (a v n) -> p a v n', a=2, v=2), [sk], ['cv'])

        for sc in range(4):
            if sc == 0:
                for tt in range(8):
                    p.dma(h[:, tt, :], x[tt * 128:(tt + 1) * 128, :], ('hx', tt), (), [('h', tt)])

            def proj_add(wsrc, gfn, actT, akey, acol0, cid0, hook=None, allow_b=True):
                for hf in range(2):
                    if (sc, cid0 + hf) in preloaded:
                        wv, wk = preloaded.pop((sc, cid0 + hf))
                    else:
                        wv, wk = load_slab(wsrc[:, hf * 512:(hf + 1) * 512].rearrange('(c q) n -> q c n', q=128), 8, gfn,
                                           cid0 + hf, sc > 0, allow_b=allow_b)
                    for tt in range(8):
                        Sb, sk = nextS()
                        for c in range(8):
                            p.mm(Sb[:, 0:512], actT[:, c, acol0 + tt * 128:acol0 + (tt + 1) * 128], wv[:, c, :],
                                 c == 0, c == 7, [wk] + akey(c, tt), [sk])
                        p.tt('dve', h[:, tt, hf * 512:(hf + 1) * 512], h[:, tt, hf * 512:(hf + 1) * 512], Sb[:, 0:512],
                             ALU.add, [sk, ('h', tt)], [('h', tt)])
                        if hf == 1 and hook is not None:
                            hook(tt)

            proj_add(w_out, (lambda c: (gsub[:, 0:1], 1.0 - LAMBDA_INIT) if c < 4 else (gsub[:, 1:2], 1.0)),
                     AT, (lambda c, tt: [('AT', c, sc * 2 + tt // 4)]), sc * 1024, 0, norm_hook)
            if sc == 0:
                mem_path()
            for hf in range(2):
                wv, wk = load_slab(w_cq[:, hf * 512:(hf + 1) * 512].rearrange('(c q) n -> q c n', q=128), 8, gmix(8),
                                   2 + hf, sc > 0, allow_b=False)
                for dtl in range(4):
                    for n in range(2):
                        Sb, sk = nextS()
                        for c in range(8):
                            p.mm(Sb[:, 0:512], wv[:, c, dtl * 128:(dtl + 1) * 128], hnT[:, c, n * 512:(n + 1) * 512],
                                 c == 0, c == 7, [wk] + [('hnT', n * 4 + i) for i in range(4)], [sk])
                        p.cp('dve' if (dtl + n) % 2 else 'act', cqT[:, hf * 4 + dtl, n * 512:(n + 1) * 512], Sb[:, 0:512],
                             [sk], [('cqT', hf * 4 + dtl, n)])
            units = [(n, hd) for n in range(2) for hd in range(4)]
            upis = {}

            def ca_S(u):
                n, hd = units[u]
                pis = []
                for mt in range(2):
                    Sb, sk = nextS()
                    for dc in range(2):
                        p.mm(Sb[:, 0:512], ckT[:, hd * 2 + dc, mt * 128:(mt + 1) * 128],
                             cqT[:, hd * 2 + dc, n * 512:(n + 1) * 512], dc == 0, dc == 1,
                             ['ckT', ('cqT', hd * 2 + dc, n)], [sk])
                    pi = rot['p'] % 4
                    rot['p'] += 1
                    p.act(PT3[pi], Sb[:, 0:512], AF.Exp, [sk], [('PT3', pi)])
                    pis.append(pi)
                upis[u] = pis

            def ca_PV(u):
                n, hd = units[u]
                pis = upis[u]
                for t in range(4):
                    for hv in range(2):
                        r_ = t * 2 + hv
                        bank, col = r_ // 3, (r_ % 3) * 129
                        for mt in range(2):
                            p.mm(PSA[bank][:, col:col + 129], PT3[pis[mt]][:, t * 128:(t + 1) * 128],
                                 cv[:, mt, hd, hv, :], (mt == 0 and r_ % 3 == 0), mt == 1,
                                 [('PT3', pis[mt]), 'cv'], [('acc', bank)], skip=True)
                for t in range(4):
                    for hv in range(2):
                        r_ = t * 2 + hv
                        bank, col = r_ // 3, (r_ % 3) * 129
                        a_ = PSA[bank][:, col:col + 129]
                        if hv == 0:
                            p.recip(rr3[:, t:t + 1], a_[:, 128:129], [('acc', bank)], [('rr3', t)])
                        p.ts('dve', cotok[:, t, hd * 256 + hv * 128:hd * 256 + (hv + 1) * 128], a_[:, 0:128],
                             rr3[:, t:t + 1], None, ALU.mult, ALU.bypass, [('acc', bank), ('rr3', t)],
                             [('cotok', t)])
                if hd == 3:
                    for t in range(4):
                        T, tk = nextT()
                        for c in range(8):
                            p.tr(T[:, c * 128:(c + 1) * 128], cotok[:, t, c * 128:(c + 1) * 128], identb[:, :],
                                 [('cotok', t), 'const'], [tk])
                        p.cp('act' if t % 2 else 'dve', hnT[:, :, (n * 4 + t) * 128:(n * 4 + t + 1) * 128],
                             T[:, :].rearrange('p (c n) -> p c n', c=8), [tk], [('hnT', n * 4 + t)])

            ca_S(0)
            for u in range(len(units)):
                if u + 1 < len(units):
                    ca_S(u + 1)
                ca_PV(u)
            proj_add(w_co, None, hnT, (lambda c, tt: [('hnT', tt)]), 0, 4, norm_hook, allow_b=False)
            for f in range(8):
                wu, wuk = load_slab(w_up[:, f * 512:(f + 1) * 512].rearrange('(c q) n -> q c n', q=128), 8, gmix(24),
                                    6 + 2 * f, sc > 0)
                wd, wdk = load_slab(w_down[f * 512:(f + 1) * 512, :].rearrange('(c q) n -> q c n', q=128), 4, None,
                                    7 + 2 * f, sc > 0)
                for fi in range(4):
                    for n in range(2):
                        Sb, sk = nextS()
                        for c in range(8):
                            p.mm(Sb[:, 0:512], wu[:, c, fi * 128:(fi + 1) * 128], hnT[:, c, n * 512:(n + 1) * 512],
                                 c == 0, c == 7, [wuk] + [('hnT', n * 4 + i) for i in range(4)], [sk])
                        k2 = (fi * 2 + n) % 2
                        rlv = rl[:, k2 * 512:(k2 + 1) * 512]
                        p.act(rlv, Sb[:, 0:512], AF.Relu, [sk], [('rl', k2)])
                        p.tt('pool' if (k2 == 0 and sc > 0) else 'dve', zT[:, fi, n * 512:(n + 1) * 512], rlv, rlv, ALU.mult, [('rl', k2)], [('zT', fi, n)])
                if f == 7 and sc < 3:
                    for hf_ in range(2):
                        preloaded[(sc + 1, hf_)] = load_slab(
                            w_out[:, hf_ * 512:(hf_ + 1) * 512].rearrange('(c q) n -> q c n', q=128), 8, None, hf_, True)
                for tt in range(8):
                    for hf in range(2):
                        Sb, sk = nextS()
                        for fi in range(4):
                            p.mm(Sb[:, 0:512], zT[:, fi, tt * 128:(tt + 1) * 128], wd[:, fi, hf * 512:(hf + 1) * 512],
                                 fi == 0, fi == 3, [wdk, ('zT', fi, tt // 4)], [sk])
                        p.tt('dve', h[:, tt, hf * 512:(hf + 1) * 512], h[:, tt, hf * 512:(hf + 1) * 512], Sb[:, 0:512],
                             ALU.add, [sk, ('h', tt)], [('h', tt)])
                        if f == 7 and hf == 1:
                            gt = sc * 8 + tt
                            p.act(rl, h[:, tt, :], AF.Square, [('h', tt)], rlk + [('ss3', tt)], accum=ss3[:, tt:tt + 1])
                            p.ts('dve', ms3[:, tt:tt + 1], ss3[:, tt:tt + 1], 1.0 / D, EPS, ALU.mult, ALU.add,
                                 [('ss3', tt)], [('ms3', tt)])
                            p.act(ln3[:, tt:tt + 1], ms3[:, tt:tt + 1], AF.Ln, [('ms3', tt)], [('ln3', tt)])
                            p.act(rs3[:, tt:tt + 1], ln3[:, tt:tt + 1], AF.Exp, [('ln3', tt)], [('rs3', tt)], scale=-0.5)
                            so = tt % 4
                            ob = obuf[:, so, :]
                            obk = [('cqT', 2 * so + a_, n_) for a_ in range(2) for n_ in range(2)]
                            p.ts('dve', ob, h[:, tt, :], rs3[:, tt:tt + 1], None, ALU.mult, ALU.bypass,
                                 [('h', tt), ('rs3', tt)], obk)
                            p.tt('pool', ob, ob, gfin, ALU.mult, obk + ['gfin'], obk)
                            p.dma(y[gt * 128:(gt + 1) * 128, :], ob, ('out', tt), obk, [('y', gt)])
                            if sc < 3:
                                gn = (sc + 1) * 8 + tt
                                p.dma(h[:, tt, :], x[gn * 128:(gn + 1) * 128, :], ('hx', tt), (), [('h', tt)])
        p.add('sp', None, [('y', gt) for gt in range(NT)], ())
        p.emit(nc, stack)
    return nc


def _consts():
    pos = np.arange(S, dtype=np.float64)
    inv_freq = 500000.0 ** (-np.arange(0, 16, 2, dtype=np.float64) / 16.0)
    ang = pos[:, None] * inv_freq[None, :]
    cos = np.cos(ang).astype(np.float32)
    sin = np.sin(ang).astype(np.float32)
    cos = np.concatenate([cos, cos], axis=1)
    sin = np.concatenate([sin, sin], axis=1)
    cosT = np.ascontiguousarray(cos.reshape(32, 128, 16).transpose(1, 0, 2).reshape(128, 512))
    sinT = np.ascontiguousarray(sin.reshape(32, 128, 16).transpose(1, 0, 2).reshape(128, 512))
    identb = np.eye(128, dtype=np.float32).astype(ml_dtypes.bfloat16)
    identf = np.eye(128, dtype=np.float32)
    k = np.arange(128)[:, None]
    q = np.arange(128)[None, :]
    negmask = np.where(k > q, -30000.0, 0.0).astype(np.float32).astype(ml_dtypes.bfloat16)
    return cosT, sinT, identb, identf, negmask


_NC_CACHE = {}
_DEBUG = {}


def in_maps_fn(shared, x, mem):
    out = []
    for b in range(8):
        m = dict(shared)
        m['x'] = x[b]
        m['mem'] = mem[b]
        out.append(m)
    return out


def kernel(x, mem, norm_mix_g, w_in, b_forget, lam_q1, lam_k1, lam_q2, lam_k2,
           diff_subln_g, fox_out_g, w_out, norm_cross_g, norm_mem_g, w_cq, w_ckv, w_co,
           norm_mlp_g, w_up, w_down, norm_final_g):
    f = lambda a: np.ascontiguousarray(np.asarray(a, dtype=np.float32))
    x = f(x)
    mem = f(mem)
    cosT, sinT, identb, identf, negmask = _consts()
    col = lambda g: f(g).reshape(8, 128).T
    gcols = np.ascontiguousarray(np.concatenate(
        [col(norm_mix_g[0]), col(norm_cross_g[0]), col(norm_mem_g[0]), col(norm_mlp_g[0])], axis=1))
    gsub = np.ascontiguousarray(np.stack([f(diff_subln_g[0]), np.tile(f(fox_out_g[0]), 2)], axis=1))
    lamv = np.ascontiguousarray(np.broadcast_to(
        np.concatenate([f(lam_q1[0]), f(lam_k1[0]), f(lam_q2[0]), f(lam_k2[0])])[None, :], (128, 256)))
    bfg = f(b_forget[0]).reshape(8, 1)
    gfin = np.ascontiguousarray(np.broadcast_to(f(norm_final_g)[None, :], (128, D)))
    shared = dict(w_in=f(w_in[0]), w_out=f(w_out[0]), w_cq=f(w_cq[0]), w_ckv=f(w_ckv[0]), w_co=f(w_co[0]),
                  w_up=f(w_up[0]), w_down=f(w_down[0]), gcols=gcols, gsub=gsub, lamv=lamv, bfg=bfg, gfin=gfin,
                  cosT=cosT, sinT=sinT, identb=identb, identf=identf, negmask=negmask)
    if _DEBUG.get('stop'):
        nc = build_nc(_DEBUG['stop'])
        res = run_bass_kernel_spmd(nc, in_maps_fn(shared, x, mem)[:_DEBUG.get('ncores', 1)],
                                   core_ids=list(range(_DEBUG.get('ncores', 1))))
        return [np.asarray(r["dbg"]) for r in res.results]
    if 'nc' not in _NC_CACHE:
        _NC_CACHE['nc'] = build_nc()
    nc = _NC_CACHE['nc']
    in_maps = []
    for b in range(8):
        m = dict(shared)
        m['x'] = x[b]
        m['mem'] = mem[b]
        in_maps.append(m)
    res = run_bass_kernel_spmd(nc, in_maps, core_ids=list(range(8)))
    return np.stack([np.asarray(r["y"], dtype=np.float32) for r in res.results], axis=0)
```
